# Optimizing a Trainium2 kernel written in Bass

```python
import jax, jax.numpy as jnp
from jax import lax
import numpy as np

D_MODEL = 1024
BATCH = 8
SEQ = 2048
DEPTH = 1

A_HEADS = 16
A_KV_HEADS = 2
A_HEAD_DIM = 64
A_GROUP = A_HEADS // A_KV_HEADS
A_WIDTH = A_HEADS * A_HEAD_DIM
A_KV_WIDTH = A_KV_HEADS * A_HEAD_DIM
WINDOW = 128
BLOCK = 128
ROPE_THETA = 10000.0

B_HEADS = 4
B_QK_WIDTH = D_MODEL // 2
B_V_WIDTH = D_MODEL
B_KEY_DIM = B_QK_WIDTH // B_HEADS
B_VAL_DIM = B_V_WIDTH // B_HEADS
GATE_RANK = 16
GATE_TAU = 16.0
CHUNK = 64

EPS = 1e-5
NEG_INF = -1e30

IN_SPLITS = (A_WIDTH, A_KV_WIDTH, A_KV_WIDTH, A_WIDTH,
             B_QK_WIDTH, B_QK_WIDTH, B_V_WIDTH, B_V_WIDTH,
             GATE_RANK,
             D_MODEL, D_MODEL)
IN_WIDTH = (2 * A_WIDTH + 2 * A_KV_WIDTH + 2 * B_QK_WIDTH + 2 * B_V_WIDTH
            + GATE_RANK + 2 * D_MODEL)

kernel_name = 'hybrid_swa_sink_gla_gated_block'


def rms_norm(x, w):
    xf = x.astype(jnp.float32)
    y = xf * lax.rsqrt(jnp.mean(xf * xf, axis=-1, keepdims=True) + EPS)
    return (y * w.astype(jnp.float32)).astype(x.dtype)


def rope(x, positions):
    half = A_HEAD_DIM // 2
    inv_freq = ROPE_THETA ** (-jnp.arange(half, dtype=jnp.float32) / half)
    ang = positions.astype(jnp.float32)[..., None] * inv_freq
    cos = jnp.cos(ang)[:, :, None, :]
    sin = jnp.sin(ang)[:, :, None, :]
    xf = x.astype(jnp.float32)
    x1, x2 = xf[..., :half], xf[..., half:]
    return jnp.concatenate([x1 * cos - x2 * sin, x2 * cos + x1 * sin], axis=-1).astype(x.dtype)


def sliding_window_attention(q, k, v, sinks):
    bsz, t = q.shape[0], q.shape[1]
    nb = t // BLOCK
    qb = q.reshape(bsz, nb, BLOCK, A_KV_HEADS, A_GROUP, A_HEAD_DIM)
    kb = k.reshape(bsz, nb, BLOCK, A_KV_HEADS, A_HEAD_DIM)
    vb = v.reshape(bsz, nb, BLOCK, A_KV_HEADS, A_HEAD_DIM)
    pad = ((0, 0), (1, 0), (0, 0), (0, 0), (0, 0))
    keys = jnp.concatenate([jnp.pad(kb, pad)[:, :-1], kb], axis=2)
    vals = jnp.concatenate([jnp.pad(vb, pad)[:, :-1], vb], axis=2)
    s = jnp.einsum('bnqhgd,bnkhd->bnhgqk', qb, keys).astype(jnp.float32) * (A_HEAD_DIM ** -0.5)
    qi = jnp.arange(BLOCK)[:, None]
    ki = jnp.arange(2 * BLOCK)[None, :]
    rel = qi + BLOCK - ki
    band = (rel >= 0) & (rel < WINDOW)
    blk = jnp.arange(nb)[:, None, None]
    valid = band[None] & ((blk > 0) | (ki >= BLOCK)[None])
    s = jnp.where(valid[None, :, None, None], s, NEG_INF)
    sink = jnp.broadcast_to(
        sinks.astype(jnp.float32).reshape(A_KV_HEADS, A_GROUP)[None, None, :, :, None, None],
        s.shape[:-1] + (1,))
    p = jax.nn.softmax(jnp.concatenate([s, sink], axis=-1), axis=-1)[..., :-1]
    o = jnp.einsum('bnhgqk,bnkhd->bnqhgd', p.astype(v.dtype), vals)
    return o.reshape(bsz, t, A_WIDTH)


def gated_linear_attention(q, k, v, log_a):
    bsz, t = q.shape[0], q.shape[1]
    n = t // CHUNK

    def chunks(a):
        return a.astype(jnp.float32).reshape(bsz, n, CHUNK, B_HEADS, -1).transpose(0, 1, 3, 2, 4)

    qc = chunks(q) * (B_KEY_DIM ** -0.5)
    kc, vc = chunks(k), chunks(v)
    b = jnp.cumsum(chunks(log_a), axis=3)
    b_last = b[:, :, :, -1:, :]
    q_e = qc * jnp.exp(b)
    k_e = kc * jnp.exp(-b)
    k_s = kc * jnp.exp(b_last - b)
    causal = jnp.tril(jnp.ones((CHUNK, CHUNK), dtype=bool))
    att = jnp.where(causal, jnp.einsum('bnhid,bnhjd->bnhij', q_e, k_e), 0.0)
    o_intra = jnp.einsum('bnhij,bnhjv->bnhiv', att, vc)
    inc = jnp.einsum('bnhjd,bnhjv->bnhdv', k_s, vc)
    decay = jnp.exp(b_last[:, :, :, 0, :])

    def step(state, inp):
        dec, dS = inp
        return dec[..., None] * state + dS, state

    s0 = jnp.zeros((bsz, B_HEADS, B_KEY_DIM, B_VAL_DIM), jnp.float32)
    _, states = lax.scan(step, s0, (decay.transpose(1, 0, 2, 3), inc.transpose(1, 0, 2, 3, 4)))
    states = states.transpose(1, 0, 2, 3, 4)
    o = o_intra + jnp.einsum('bnhid,bnhdv->bnhiv', q_e, states)
    return o.transpose(0, 1, 3, 2, 4).reshape(bsz, t, B_HEADS, B_VAL_DIM)


def setup_inputs(seed: int = 0) -> dict:
    key = jax.random.key(seed)
    ks = jax.random.split(key, 13)
    f32 = jnp.float32

    def lin(k, shape, fan_in):
        return jax.random.normal(k, shape, f32) * (fan_in ** -0.5)

    x = jax.random.normal(ks[0], (BATCH, SEQ, D_MODEL), f32)
    positions = jnp.broadcast_to(jnp.arange(SEQ, dtype=jnp.int32)[None, :], (BATCH, SEQ))
    norm_w = 1.0 + 0.02 * jax.random.normal(ks[1], (DEPTH, D_MODEL), f32)
    w_in = lin(ks[2], (DEPTH, D_MODEL, IN_WIDTH), D_MODEL)
    a_sinks = 0.5 * jax.random.normal(ks[3], (DEPTH, A_HEADS), f32)
    b_gate_up = lin(ks[4], (DEPTH, GATE_RANK, B_QK_WIDTH), GATE_RANK)
    b_gate_bias = 0.1 * jax.random.normal(ks[5], (DEPTH, B_QK_WIDTH), f32)
    b_out_norm_w = 1.0 + 0.02 * jax.random.normal(ks[6], (DEPTH, B_VAL_DIM), f32)
    w_a_proj = lin(ks[7], (DEPTH, A_WIDTH, D_MODEL), A_WIDTH)
    w_b_proj = lin(ks[8], (DEPTH, B_V_WIDTH, D_MODEL), B_V_WIDTH)
    w_out = lin(ks[9], (DEPTH, D_MODEL, D_MODEL), D_MODEL)
    final_norm_w = 1.0 + 0.02 * jax.random.normal(ks[10], (D_MODEL,), f32)
    return {'x': x, 'positions': positions, 'norm_w': norm_w, 'w_in': w_in,
            'a_sinks': a_sinks, 'b_gate_up': b_gate_up, 'b_gate_bias': b_gate_bias,
            'b_out_norm_w': b_out_norm_w, 'w_a_proj': w_a_proj, 'w_b_proj': w_b_proj,
            'w_out': w_out, 'final_norm_w': final_norm_w}


def reference(x, positions, norm_w, w_in, a_sinks, b_gate_up, b_gate_bias,
              b_out_norm_w, w_a_proj, w_b_proj, w_out, final_norm_w):
    bsz, t = x.shape[0], x.shape[1]
    offsets = [int(o) for o in np.cumsum(IN_SPLITS)[:-1]]
    for layer in range(DEPTH):
        h = rms_norm(x, norm_w[layer])
        proj = jnp.einsum('btd,de->bte', h, w_in[layer])
        (a_q, a_k, a_v, a_gate, b_q, b_k, b_v, b_gate, b_low,
         m_a, m_b) = jnp.split(proj, offsets, axis=-1)

        q = rope(a_q.reshape(bsz, t, A_HEADS, A_HEAD_DIM), positions)
        k = rope(a_k.reshape(bsz, t, A_KV_HEADS, A_HEAD_DIM), positions)
        v = a_v.reshape(bsz, t, A_KV_HEADS, A_HEAD_DIM)
        o_a = sliding_window_attention(q, k, v, a_sinks[layer]) * jax.nn.silu(a_gate)
        y_a = jnp.einsum('bte,ed->btd', o_a, w_a_proj[layer])

        gk = jnp.einsum('btr,re->bte', b_low, b_gate_up[layer]) + b_gate_bias[layer]
        log_a = jax.nn.log_sigmoid(gk.astype(jnp.float32)) / GATE_TAU
        o_b = gated_linear_attention(b_q.reshape(bsz, t, B_HEADS, B_KEY_DIM),
                                     b_k.reshape(bsz, t, B_HEADS, B_KEY_DIM),
                                     b_v.reshape(bsz, t, B_HEADS, B_VAL_DIM),
                                     log_a.reshape(bsz, t, B_HEADS, B_KEY_DIM))
        o_b = rms_norm(o_b.astype(x.dtype), b_out_norm_w[layer]).reshape(bsz, t, B_V_WIDTH)
        o_b = o_b * jax.nn.silu(b_gate)
        y_b = jnp.einsum('bte,ed->btd', o_b, w_b_proj[layer])

        merged = jax.nn.sigmoid(m_a) * y_a + jax.nn.sigmoid(m_b) * y_b
        x = x + jnp.einsum('btd,de->bte', merged, w_out[layer])
    return rms_norm(x, final_norm_w)
```

```python
import contextlib
import math

import numpy as np
import concourse.bass as bass
import concourse.mybir as mybir
from concourse.bass_utils import run_bass_kernel_spmd

F32 = mybir.dt.float32
BF16 = mybir.dt.bfloat16
I32 = mybir.dt.int32
AF = mybir.ActivationFunctionType
ALU = mybir.AluOpType

T = 2048
D = 1024
NBLK = 16
NHALF = 2
BPH = NBLK // NHALF
WCH = 4
EPS = 1e-5
O_AQ, O_AK, O_AV, O_AG = 0, 1024, 1152, 1280
O_BQ, O_BK, O_BV, O_BG = 2304, 2816, 3328, 4352
O_LOW, O_MA, O_MB = 5376, 5392, 6416
IN_W = 7440


class Res:
    __slots__ = ("name", "w", "r")

    def __init__(self, name):
        self.name = name
        self.w = None
        self.r = {}


class Prog:
    ENGS = ("sync", "scalar", "vector", "gpsimd", "tensor")

    def __init__(self, nc):
        self.nc = nc
        self.lists = {e: [] for e in self.ENGS}
        self.sems = {}
        self.count = {}
        self.waited = {e: {} for e in self.ENGS}
        self.sem_ctx = []

    def _sem(self, chan):
        if chan not in self.sems:
            cm = self.nc.semaphore("s_" + chan)
            s = cm.__enter__()
            self.sem_ctx.append(cm)
            self.sems[chan] = s
            self.count[chan] = 0
        return self.sems[chan]

    def op(self, eng, fn, reads=(), writes=(), chan=None):
        if chan is None:
            chan = eng
        step = 1 if chan == eng else 16
        need = {}
        for r in reads:
            if r.w is not None:
                c, v = r.w
                need[c] = max(need.get(c, 0), v)
        for w in writes:
            if w.w is not None:
                c, v = w.w
                need[c] = max(need.get(c, 0), v)
            for c, v in w.r.items():
                need[c] = max(need.get(c, 0), v)
        waits = []
        for c, v in need.items():
            if c == eng and eng == "tensor":
                continue
            if self.waited[eng].get(c, 0) < v:
                self.waited[eng][c] = v
                waits.append((self.sems[c], v))
        sem = self._sem(chan)
        self.count[chan] += step
        val = self.count[chan]
        self.lists[eng].append((waits, fn, sem, step))
        for r in reads:
            r.r[chan] = max(r.r.get(chan, 0), val)
        for w in writes:
            w.w = (chan, val)
            w.r = {}
        return val

    def final_wait(self, eng, chans):
        waits = [(self.sems[c], self.count[c]) for c in chans if c in self.sems]
        self.lists[eng].append((waits, None, None, None))

    def emit(self):
        with self.nc.Block() as block:
            def mk(ename):
                def body(e):
                    for waits, fn, sem, step in self.lists[ename]:
                        for s, v in waits:
                            e.wait_ge(s, v)
                        if fn is not None:
                            fn(e).then_inc(sem, step)
                return body
            for ename in self.ENGS:
                if self.lists[ename]:
                    getattr(block, ename)(mk(ename))
        for cm in reversed(self.sem_ctx):
            cm.__exit__(None, None, None)


def build(dbg=None):
    nc = bass.Bass("TRN2", target_bir_lowering=False)

    def din(name, shape, dt=F32):
        return nc.dram_tensor(name, shape, dt, kind="ExternalInput").ap()

    x = din("x", [T, D])
    pos = din("pos", [128, NBLK], I32)
    invf = din("invf", [128, 32])
    norm_w = din("norm_w", [1, D])
    fnorm_w = din("fnorm_w", [1, D])
    bnw = din("bnw", [1, 256])
    sinks = din("sinks", [1, 16])
    gup = din("gup", [17, 512])
    w_in = din("w_in", [D, IN_W])
    w_a = din("w_a", [D, D])
    w_b = din("w_b", [D, D])
    w_o = din("w_o", [D, D])
    out = nc.dram_tensor("out", [T, D], F32, kind="ExternalOutput").ap()
    WNAMES = ("q", "ga", "ma", "wa", "qkb", "vb", "gb", "mb", "wb", "wo")
    wscr = {nm: nc.dram_tensor("wscr_" + nm, [128, 8, D], BF16, kind="Internal").ap() for nm in WNAMES}
    dbg_out = {}
    if dbg:
        for nm, shp in (("d_hT", [128, 8, 1024]), ("d_b", [128, 8, 512]), ("d_ga", [128, 8, 1024]),
                        ("d_dec", [128, 32]), ("d_cos", [128, 16, 32]), ("d_sin", [128, 16, 32])):
            dbg_out[nm] = nc.dram_tensor(nm, shp, F32, kind="ExternalOutput").ap()

    w_in_v = w_in.rearrange("(c p) n -> p c n", p=128)
    w_a_v = w_a.rearrange("(c p) n -> p c n", p=128)
    w_b_v = w_b.rearrange("(c p) n -> p c n", p=128)
    w_o_v = w_o.rearrange("(c p) n -> p c n", p=128)

    with contextlib.ExitStack() as es:
        def sb(name, shape, dt):
            return es.enter_context(nc.sbuf_tensor(name, shape, dt))

        def ps(name, shape, dt):
            return es.enter_context(nc.psum_tensor(name, shape, dt))

        P = Prog(nc)
        R = {}

        def res(name):
            if name not in R:
                R[name] = Res(name)
            return R[name]

        ident = sb("ident", [128, 128], BF16)
        triL = sb("triL", [128, 128], F32)
        mask2 = sb("mask2", [128, 2, 128], BF16)
        neghalf = sb("neghalf", [128, 8], F32)
        normw_bc = sb("normw_bc", [128, 1, D], F32)
        fnw_bc = sb("fnw_bc", [128, 1, D], F32)
        bnw_bc = sb("bnw_bc", [128, 1, 256], F32)
        sink_bc = sb("sink_bc", [128, 1, 16], F32)
        sinkexp = sb("sinkexp", [128, 16], F32)
        invf_sb = sb("invf_sb", [128, 32], F32)
        pos_i = sb("pos_i", [128, NBLK], I32)
        pos_f = sb("pos_f", [128, NBLK], F32)
        cos_t = sb("cos_t", [128, NBLK, 32], F32)
        sin_t = sb("sin_t", [128, NBLK, 32], F32)
        gup_sb = sb("gup_sb", [17, 512], F32)
        wlow = sb("wlow", [128, 8, 16], BF16)
        decay = sb("decay", [128, BPH * 4], F32)

        hT = sb("hT", [128, 8, BPH * 128], BF16)
        b_all = sb("b_all", [128, BPH, 512], F32)
        ga = sb("ga", [128, BPH, D], BF16)
        S_f = sb("S_f", [128, 4, 256], F32)
        S_b = sb("S_b", [128, 4, 256], BF16)
        kT2 = [sb(f"kT2_{i}", [128, 2, 128], BF16) for i in range(3)]
        vaug = [sb(f"vaug_{i}", [128, 2, 66], BF16) for i in range(3)]

        WS = [sb(f"ws{i}", [128, 8, D], BF16) for i in range(5)]
        WKV = sb("wkv", [128, 8, 256], BF16)
        R_WS = [[Res(f"ws{i}_{c}") for c in range(8)] for i in range(5)]
        R_WKV = [Res(f"wkv_{c}") for c in range(8)]

        FS = [sb(f"fs{i}", [128, D], F32) for i in range(4)]
        HS = [sb(f"hs{i}", [128, D], BF16) for i in range(4)]
        SG = [sb(f"sg{i}", [128, D], BF16) for i in range(2)]
        TM = [sb(f"tm{i}", [128, D], BF16) for i in range(2)]
        OA = [sb(f"oa{i}", [128, D], BF16) for i in range(2)]
        OGT = sb("ogt", [128, 2, 256], F32)
        fTq = [sb(f"fTq{i}", [128, 8, 128], BF16) for i in range(2)]
        fTo = sb("fTo", [128, 8, 128], BF16)
        p_sb = [sb(f"p_sb{i}", [128, 2, 2, 2, 128], BF16) for i in range(2)]
        kdup = sb("kdup", [128, 2, 2, 64], BF16)
        small = sb("small", [128, 64], F32)
        R_small = [Res(f"small{i}") for i in range(10)]
        blT = FS[3][0:17, :]

        bigs = [ps("bigA", [128, D], F32), ps("bigB", [128, D], F32)]
        R_bigs = [Res("bigA"), Res("bigB")]
        pk4 = ps("pk4", [128, 512], F32)
        tp5f = ps("tp5", [128, 512], F32)
        tp5 = tp5f[:].bitcast(BF16).rearrange("p (c t) -> p c t", t=128)
        ps67 = ps("ps67", [128, 1024], F32)
        ps6 = ps67[:, 0:512]
        ps7 = ps67[:, 512:1024]
        R_pk4, R_tp5, R_ps6, R_ps7 = [Res(n) for n in "pk4 tp5 ps6 ps7".split()]
        bigtog = [0]

        def nextbig():
            k = bigtog[0]
            bigtog[0] ^= 1
            return bigs[k], R_bigs[k]

        R_FS = [Res("fs0"), Res("fs1"), Res("fs2"), Res("fs3")]
        R_HS = [Res(f"hs{i}") for i in range(4)]
        R_SG = [Res("sg0"), Res("sg1")]
        R_TM = [Res("tm0"), Res("tm1")]
        R_OA = [Res("oa0"), Res("oa1")]
        R_OGT = [Res("ogt0"), Res("ogt1")]
        R_fTq = [Res("fTq0"), Res("fTq1")]
        R_fTo = Res("fTo")
        R_psb = [Res("psb0"), Res("psb1")]
        R_kdup = Res("kdup")
        R_kT2 = [Res("kT2_0"), Res("kT2_1"), Res("kT2_2")]
        R_vaug = [res("vaug0"), res("vaug1"), res("vaug2")]
        R_hT = [Res(f"hT{i}") for i in range(BPH)]
        R_b = [Res(f"b{i}") for i in range(BPH)]
        R_ga = [Res(f"ga{i}") for i in range(BPH)]
        R_blT = R_FS[3]
        R_decay = Res("decay")
        R_Sf = res("S_f")
        R_Sb = res("S_b")

        const_res = [res(n) for n in ("normw", "fnw", "bnw", "sink", "invf", "pos", "gup")]
        for b_ in range(2):
            P.op("sync", (lambda b_: lambda e: e.dma_start(out=FS[b_][:], in_=x[b_ * 128:(b_ + 1) * 128, :]))(b_),
                 writes=[R_FS[b_]], chan=f"fsx{b_}")
        P.op("sync", lambda e: e.dma_start(out=pos_i[:], in_=pos), chan="const")
        P.op("sync", lambda e: e.dma_start(out=invf_sb[:], in_=invf), chan="const")
        P.op("sync", lambda e: e.dma_start(out=normw_bc[:], in_=norm_w.partition_broadcast(128)), chan="const")
        P.op("sync", lambda e: e.dma_start(out=sink_bc[:], in_=sinks.partition_broadcast(128)), chan="const")
        P.op("sync", lambda e: e.dma_start(out=gup_sb[:], in_=gup), chan="const")
        P.op("sync", lambda e: e.dma_start(out=bnw_bc[:], in_=bnw.partition_broadcast(128)), chan="const")
        P.op("sync", lambda e: e.dma_start(out=fnw_bc[:], in_=fnorm_w.partition_broadcast(128)), chan="const")
        for r_ in const_res:
            r_.w = ("const", P.count["const"])
        P.op("gpsimd", lambda e: e.dma_start(out=wlow[:], in_=w_in_v[:, :, O_LOW:O_LOW + 16]),
             writes=[res("wlow")], chan="wlow")

        st = HS[0][:].bitcast(F32)
        onesf, negs, maskc_f, maskp_f = st[:, 0:128], st[:, 128:256], st[:, 256:384], st[:, 384:512]
        identf = HS[1][:].bitcast(F32)[:, 0:128]

        def setup_pool0(e):
            e.memset(onesf, 1.0)
            e.memset(negs, -1.0 / 16.0)
            e.memset(neghalf[:], -0.5)
            e.memset(S_f[:], 0.0)
            e.memset(S_b[:], 0.0)
            e.memset(vaug[0][:], 1.0)
            e.memset(vaug[1][:], 1.0)
            return e.memset(vaug[2][:], 1.0)
        P.op("gpsimd", setup_pool0, writes=[res("cm"), res("S_f"), res("S_b"), res("vaug0"), res("vaug1"), res("vaug2"), R_HS[0]])

        def setup_pool(e):
            e.affine_select(out=identf, in_=onesf, pattern=[[1, 128]], compare_op=ALU.is_equal,
                            fill=0.0, base=0, channel_multiplier=-1)
            e.affine_select(out=maskc_f, in_=onesf, pattern=[[1, 128]], compare_op=ALU.is_ge,
                            fill=0.0, base=0, channel_multiplier=-1)
            e.affine_select(out=maskp_f, in_=onesf, pattern=[[-1, 128]], compare_op=ALU.is_gt,
                            fill=0.0, base=0, channel_multiplier=1)
            return e.affine_select(out=triL[:], in_=negs, pattern=[[1, 128]], compare_op=ALU.is_ge,
                                   fill=0.0, base=0, channel_multiplier=-1)
        P.op("gpsimd", setup_pool, reads=[res("cm"), R_HS[0]], writes=[res("c0"), R_HS[0], R_HS[1]])

        def setup_pool2(e):
            e.tensor_copy(out=ident[:], in_=identf)
            e.tensor_copy(out=mask2[:, 0, :], in_=maskp_f)
            return e.tensor_copy(out=mask2[:, 1, :], in_=maskc_f)
        P.op("gpsimd", setup_pool2, reads=[res("c0"), R_HS[0], R_HS[1]], writes=[res("c1"), R_HS[0], R_HS[1]])
        CONSTS = [res("c0"), res("c1"), res("cm")]

        ROPE = [res("sin"), res("cos")]

        def emit_rope_tables():
            TWO_PI = 2.0 * math.pi
            C1 = 6.28125
            C2 = TWO_PI - C1
            PI_LO = 3.1415925
            ang = FS[0][:, 0:512].rearrange("p (n j) -> p n j", j=32)
            uu = FS[0][:, 512:1024].rearrange("p (n j) -> p n j", j=32)
            ki = FS[1][:, 0:512].bitcast(I32).rearrange("p (n j) -> p n j", j=32)
            kf = FS[1][:, 512:1024].rearrange("p (n j) -> p n j", j=32)
            rr = FS[2][:, 0:512].rearrange("p (n j) -> p n j", j=32)
            rc = FS[2][:, 512:1024].rearrange("p (n j) -> p n j", j=32)
            P.op("vector", lambda e: e.tensor_copy(out=pos_f[:], in_=pos_i[:]), reads=[res("pos")], writes=[res("pos_f")])
            P.op("vector", lambda e: e.tensor_tensor(out=ang, in0=pos_f[:].unsqueeze(2).to_broadcast([128, NBLK, 32]),
                                                     in1=invf_sb[:].unsqueeze(1).to_broadcast([128, NBLK, 32]), op=ALU.mult),
                 reads=[res("pos_f"), res("invf")], writes=[R_FS[0]])
            P.op("vector", lambda e: e.tensor_scalar(out=uu, in0=ang, scalar1=1.0 / TWO_PI, scalar2=None, op0=ALU.mult),
                 reads=[R_FS[0]], writes=[R_FS[0]])
            P.op("vector", lambda e: e.tensor_copy(out=ki, in_=uu), reads=[R_FS[0]], writes=[R_FS[1]])
            P.op("vector", lambda e: e.tensor_copy(out=kf, in_=ki), reads=[R_FS[1]], writes=[R_FS[1]])
            P.op("vector", lambda e: e.scalar_tensor_tensor(out=rr, in0=kf, scalar=-C1, in1=ang, op0=ALU.mult, op1=ALU.add),
                 reads=[R_FS[1], R_FS[0]], writes=[R_FS[2]])
            P.op("vector", lambda e: e.scalar_tensor_tensor(out=rr, in0=kf, scalar=-C2, in1=rr, op0=ALU.mult, op1=ALU.add),
                 reads=[R_FS[1], R_FS[2]], writes=[R_FS[2]])
            P.op("vector", lambda e: e.tensor_scalar(out=rc, in0=rr, scalar1=math.pi / 2, scalar2=None, op0=ALU.add),
                 reads=[R_FS[2]], writes=[R_FS[2]])
            P.op("vector", lambda e: e.tensor_single_scalar(out=uu, in_=rc, scalar=math.pi, op=ALU.is_gt),
                 reads=[R_FS[2]], writes=[R_FS[0]])
            P.op("vector", lambda e: e.scalar_tensor_tensor(out=rc, in0=uu, scalar=-TWO_PI, in1=rc, op0=ALU.mult, op1=ALU.add),
                 reads=[R_FS[0], R_FS[2]], writes=[R_FS[2]])
            P.op("vector", lambda e: e.tensor_scalar(out=FS[2][:], in0=FS[2][:], scalar1=-PI_LO, scalar2=PI_LO, op0=ALU.max, op1=ALU.min),
                 reads=[R_FS[2]], writes=[R_FS[2]])
            P.op("scalar", lambda e: e.activation(out=sin_t[:], in_=rr, func=AF.Sin), reads=[R_FS[2]], writes=[res("sin")])
            P.op("scalar", lambda e: e.activation(out=cos_t[:], in_=rc, func=AF.Sin), reads=[R_FS[2]], writes=[res("cos")])
            P.op("scalar", lambda e: e.activation(out=sinkexp[:], in_=sink_bc[:, 0, :], func=AF.Exp),
                 reads=[res("sink")], writes=[res("sinkexp")])


        free_slots = [0, 1, 2, 3, 4]
        W = {}

        def load_group(tile, rlist, src_view, col0, ncols, chan, extra_reads=()):
            for c0 in range(0, 8, WCH):
                P.op("gpsimd",
                     (lambda c0: lambda e: e.dma_start(out=tile[:, c0:c0 + WCH, 0:ncols], in_=src_view[:, c0:c0 + WCH, col0:col0 + ncols]))(c0),
                     reads=list(extra_reads), writes=rlist[c0:c0 + WCH], chan=chan)
            for c in range(8):
                rlist[c].w = (chan, P.count[chan])

        R_scr = {nm: Res("scr_" + nm) for nm in WNAMES}
        pending_store = []
        stored = set()

        def wload(name, src_view, col0, extra_reads=()):
            k = free_slots.pop(0)
            if name in stored:
                for c0 in range(0, 8, WCH):
                    P.op("sync",
                         (lambda c0: lambda e: e.dma_start(out=WS[k][:, c0:c0 + WCH, :], in_=wscr[name][:, c0:c0 + WCH, :]))(c0),
                         reads=[R_scr[name]], writes=R_WS[k][c0:c0 + WCH], chan=f"wh{k}")
                for c in range(8):
                    R_WS[k][c].w = (f"wh{k}", P.count[f"wh{k}"])
            else:
                load_group(WS[k], R_WS[k], src_view, col0, 1024, f"ws{k}", extra_reads)
                pending_store.append(name)
            W[name] = k

        def flush_stores():
            while pending_store:
                name = pending_store.pop(0)
                k = W[name]
                P.op("sync", (lambda name, k: lambda e: e.dma_start(out=wscr[name], in_=WS[k][:]))(name, k),
                     reads=R_WS[k], writes=[R_scr[name]], chan="scr_" + name)
                stored.add(name)

        def wfree(name):
            free_slots.append(W.pop(name))

        def wt(name):
            return WS[W[name]], R_WS[W[name]]

        def tm_proj(ps_tile, ps_res, lhs, lhs_sel, lhs_res, Wt, W_res, wcol0, ncols):
            def fn(e):
                last = None
                for j0 in range(0, ncols, 512):
                    w_ = min(512, ncols - j0)
                    for c in range(8):
                        last = e.matmul(ps_tile[:, j0:j0 + w_], lhsT=lhs[:, c, lhs_sel],
                                        rhs=Wt[:, c, wcol0 + j0:wcol0 + j0 + w_], start=(c == 0), stop=(c == 7))
                return last
            P.op("tensor", fn, reads=list(lhs_res) + list(W_res) + CONSTS, writes=[ps_res])

        def transposes(src, src_res, nchunks, dst, dst_res, dst_sel=None, chunk0=0):
            def fn(e):
                last = None
                for k in range(nchunks):
                    last = e.transpose(out=tp5[:, chunk0 + k, :], in_=src[:, k * 128:(k + 1) * 128], identity=ident[:])
                return last
            P.op("tensor", fn, reads=list(src_res) + CONSTS, writes=[R_tp5])
            if dst_sel is None:
                o_ = dst[:, chunk0:chunk0 + nchunks, :]
            else:
                o_ = dst[:, chunk0:chunk0 + nchunks, dst_sel]
            P.op("scalar", lambda e: e.activation(out=o_, in_=tp5[:, chunk0:chunk0 + nchunks, :], func=AF.Copy),
                 reads=[R_tp5], writes=list(dst_res))

        def rstd_pool(ss_ap, k, inv_n, tmp_ap, out_ap, r_ss, r_tmp, r_out):
            P.op("vector", lambda e: e.tensor_scalar(out=tmp_ap, in0=ss_ap, scalar1=inv_n, scalar2=EPS, op0=ALU.mult, op1=ALU.add),
                 reads=[r_ss], writes=[r_tmp])
            P.op("gpsimd", lambda e: e.tensor_tensor(out=out_ap, in0=tmp_ap, in1=neghalf[:, 0:k], op=ALU.pow),
                 reads=[r_tmp] + CONSTS, writes=[r_out])

        def dump_and_finish():
            allres = list(R_hT) + list(R_b) + list(R_ga) + [R_decay] + ROPE
            P.op("gpsimd", lambda e: e.dma_start(out=dbg_out["d_hT"], in_=hT[:]), reads=allres, chan="dbg")
            P.op("gpsimd", lambda e: e.dma_start(out=dbg_out["d_b"], in_=b_all[:]), reads=allres, chan="dbg")
            P.op("gpsimd", lambda e: e.dma_start(out=dbg_out["d_ga"], in_=ga[:]), reads=allres, chan="dbg")
            P.op("gpsimd", lambda e: e.dma_start(out=dbg_out["d_dec"], in_=decay[:]), reads=allres, chan="dbg")
            P.op("gpsimd", lambda e: e.dma_start(out=dbg_out["d_cos"], in_=cos_t[:]), reads=allres, chan="dbg")
            P.op("gpsimd", lambda e: e.dma_start(out=dbg_out["d_sin"], in_=sin_t[:]), reads=allres, chan="dbg")
            P.final_wait("gpsimd", ["dbg"])
            P.final_wait("sync", ["fsx0", "fsx1", "fsx2", "fsx3"])
            P.emit()

        R_p1 = [Res(f"p1s{k}") for k in range(9)]

        p1_loaded = {(0, 0), (0, 1)}

        def p1_load(H, i, slots=(0, 1, 2), q="sync"):
            n = H * BPH + i
            xs = slots[i % len(slots)]
            p1_loaded.add((H, i))
            P.op(q, lambda e: e.dma_start(out=FS[xs][:], in_=x[n * 128:(n + 1) * 128, :]),
                 writes=[R_FS[xs]], chan=(f"fsx{xs}" if q == "sync" else f"fsg{xs}"))

        def p1_head(H, i, slots=(0, 1, 2), q="sync"):
            xs = slots[i % len(slots)]
            s = i % 3
            b2 = i % 2
            if (H, i) not in p1_loaded:
                p1_load(H, i, slots, q)
            P.op("scalar", lambda e: e.activation(out=HS[2 + b2][:], in_=FS[xs][:], func=AF.Square, accum_out=small[:, 40 + s:41 + s]),
                 reads=[R_FS[xs]], writes=[R_HS[2 + b2], R_p1[s]])
            P.op("scalar", lambda e: e.activation(out=small[:, 43 + s:44 + s], in_=small[:, 40 + s:41 + s], func=AF.Ln, scale=1.0 / D, bias=epsb[:, 0:1]),
                 reads=[R_p1[s], res("epsb")], writes=[R_p1[3 + s]])
            P.op("scalar", lambda e: e.activation(out=small[:, 46 + s:47 + s], in_=small[:, 43 + s:44 + s], func=AF.Exp, scale=-0.5),
                 reads=[R_p1[3 + s]], writes=[R_p1[6 + s]])

        def p1_tail(H, i, slots=(0, 1, 2)):
            xs = slots[i % len(slots)]
            s = i % 3
            b2 = i % 2
            P.op("vector", lambda e: e.scalar_tensor_tensor(out=HS[b2][:], in0=FS[xs][:], scalar=small[:, 46 + s:47 + s],
                                                            in1=normw_bc[:, 0, :], op0=ALU.mult, op1=ALU.mult),
                 reads=[R_FS[xs], R_p1[6 + s], res("normw")], writes=[R_HS[b2]])
            transposes(HS[b2], [R_HS[b2]], 8, hT, [R_hT[i]], dst_sel=slice(i * 128, (i + 1) * 128))

        def a_qproj(i):
            n_, sl = i, i % 2
            tsel = slice(i * 128, (i + 1) * 128)
            bg, Rbg = nextbig()
            Wt, Wr = wt("q")
            tm_proj(bg, Rbg, hT, tsel, [R_hT[i]], Wt, Wr, 0, 1024)
            return bg, Rbg

        def a_rope_q(n, bg, Rbg):
            q4 = bg[:].rearrange("p (h t j) -> p h t j", t=2, j=32)
            tA4 = FS[0][:].rearrange("p (h t j) -> p h t j", t=2, j=32)
            tB4 = FS[1][:].rearrange("p (h t j) -> p h t j", t=2, j=32)
            cosb = cos_t[:, n, :]
            sinb = sin_t[:, n, :]
            P.op("vector", lambda e: e.tensor_tensor(out=tA4, in0=q4, in1=cosb.unsqueeze(1).unsqueeze(1).to_broadcast([128, 16, 2, 32]), op=ALU.mult),
                 reads=[Rbg] + ROPE, writes=[R_FS[0]])

            def fn(e):
                e.scalar_tensor_tensor(out=tB4[:, :, 0, :], in0=q4[:, :, 1, :], scalar=-1.0, in1=sinb.unsqueeze(1).to_broadcast([128, 16, 32]),
                                       op0=ALU.mult, op1=ALU.mult)
                return e.tensor_tensor(out=tB4[:, :, 1, :], in0=q4[:, :, 0, :], in1=sinb.unsqueeze(1).to_broadcast([128, 16, 32]), op=ALU.mult)
            P.op("vector", fn, reads=[Rbg] + ROPE, writes=[R_FS[1]])
            P.op("gpsimd", lambda e: e.tensor_tensor(out=HS[0][:], in0=FS[0][:], in1=FS[1][:], op=ALU.add),
                 reads=[R_FS[0], R_FS[1]], writes=[R_HS[0]])

        def a_kv(i, n):
            sl = n % 3
            tsel = slice(i * 128, (i + 1) * 128)
            kvp, Rkvp = nextbig()
            tm_proj(kvp, Rkvp, hT, tsel, [R_hT[i]], WKV, R_WKV, 0, 256)
            k4 = kvp[:, 0:128].rearrange("p (g t j) -> p g t j", t=2, j=32)
            tAk = OGT[:, 0, 0:128].rearrange("p (g t j) -> p g t j", t=2, j=32)
            tBk = OGT[:, 0, 128:256].rearrange("p (g t j) -> p g t j", t=2, j=32)
            cosb = cos_t[:, n, :]
            sinb = sin_t[:, n, :]

            def fn(e):
                e.tensor_tensor(out=tAk, in0=k4, in1=cosb.unsqueeze(1).unsqueeze(1).to_broadcast([128, 2, 2, 32]), op=ALU.mult)
                e.scalar_tensor_tensor(out=tBk[:, :, 0, :], in0=k4[:, :, 1, :], scalar=-1.0, in1=sinb.unsqueeze(1).to_broadcast([128, 2, 32]),
                                       op0=ALU.mult, op1=ALU.mult)
                e.tensor_tensor(out=tBk[:, :, 1, :], in0=k4[:, :, 0, :], in1=sinb.unsqueeze(1).to_broadcast([128, 2, 32]), op=ALU.mult)
                return e.tensor_copy(out=vaug[sl][:, :, 0:64], in_=kvp[:, 128:256].rearrange("p (g d) -> p g d", d=64))
            P.op("vector", fn, reads=[Rkvp] + ROPE, writes=[R_OGT[0], R_vaug[sl]])

            def fn(e):
                a_ = OGT[:, 0, 0:128].rearrange("p (g d) -> p g d", d=64)
                b_ = OGT[:, 0, 128:256].rearrange("p (g d) -> p g d", d=64)
                e.tensor_tensor(out=kdup[:, :, 0, :], in0=a_, in1=b_, op=ALU.add)
                return e.tensor_tensor(out=kdup[:, :, 1, :], in0=a_, in1=b_, op=ALU.add)
            P.op("vector", fn, reads=[R_OGT[0]], writes=[R_kdup])

        def a_gate(i):
            tsel = slice(i * 128, (i + 1) * 128)
            bg, Rbg = nextbig()
            Wt, Wr = wt("ga")
            tm_proj(bg, Rbg, hT, tsel, [R_hT[i]], Wt, Wr, 0, 1024)
            P.op("scalar", lambda e: e.activation(out=FS[2][:], in_=bg[:], func=AF.Tanh, scale=0.5),
                 reads=[Rbg], writes=[R_FS[2]])
            P.op("vector", lambda e: e.scalar_tensor_tensor(out=SG[i % 2][:], in0=FS[2][:], scalar=1.0, in1=bg[:], op0=ALU.add, op1=ALU.mult),
                 reads=[R_FS[2], Rbg], writes=[R_SG[i % 2]])

        def a_mproj(i):
            tsel = slice(i * 128, (i + 1) * 128)
            bg, Rbg = nextbig()
            Wt, Wr = wt("ma")
            tm_proj(bg, Rbg, hT, tsel, [R_hT[i]], Wt, Wr, 0, 1024)
            P.op("scalar", lambda e: e.activation(out=TM[i % 2][:], in_=bg[:], func=AF.Tanh, scale=0.5),
                 reads=[Rbg], writes=[R_TM[i % 2]])

        def a_qT(i, n):
            sl = n % 2
            transposes(HS[0], [R_HS[0]], 8, fTq[sl], [R_fTq[sl]])

        def a_kT(i, n):
            sl = n % 3

            def fn(e):
                e.transpose(out=tp5[:, 0, :], in_=kdup[:, 0, :, :].rearrange("p r d -> p (r d)"), identity=ident[:])
                return e.transpose(out=tp5[:, 1, :], in_=kdup[:, 1, :, :].rearrange("p r d -> p (r d)"), identity=ident[:])
            P.op("tensor", fn, reads=[R_kdup] + CONSTS, writes=[R_tp5])
            P.op("scalar", lambda e: e.activation(out=kT2[sl][:], in_=tp5[:, 0:2, :], func=AF.Copy),
                 reads=[R_tp5], writes=[R_kT2[sl]])

        def a_qkT(i, n):
            a_qT(i, n)
            a_kT(i, n)

        def a_scores(n, qd):
            sl = n % 2
            g = qd // 2
            u = qd % 2
            h0 = qd * 4
            kbs = [1] if n == 0 else [0, 1]
            S5 = ps67[:].rearrange("p (par kb j q) -> p par kb j q", par=2, kb=2, j=2)

            def fn(e):
                last = None
                for kb in kbs:
                    ksl = n % 3 if kb == 1 else (n - 1) % 3
                    for j in range(2):
                        for par in range(2):
                            h = h0 + 2 * j + par
                            c = h // 2
                            lo = par * 64
                            last = e.matmul(S5[:, par, kb, j, :], lhsT=kT2[ksl][lo:lo + 64, g, :], rhs=fTq[sl][lo:lo + 64, c, :],
                                            start=True, stop=True)
                return last
            P.op("tensor", fn, reads=[R_kT2[n % 3], R_kT2[(n - 1) % 3], R_fTq[sl]], writes=[R_ps6, R_ps7])
            if n == 0:
                P.op("scalar", lambda e: e.activation(out=p_sb[u][:, :, 1, :, :], in_=S5[:, :, 1, :, :], func=AF.Exp, scale=0.125),
                     reads=[R_ps6, R_ps7], writes=[R_psb[u]])

                def fn(e):
                    last = None
                    for par in range(2):
                        last = e.tensor_tensor(out=p_sb[u][:, par, 1, :, :], in0=p_sb[u][:, par, 1, :, :],
                                               in1=mask2[:, 1, :].unsqueeze(1).to_broadcast([128, 2, 128]), op=ALU.mult)
                    return last
                P.op("vector", fn, reads=[R_psb[u]] + CONSTS, writes=[R_psb[u]])
            else:
                P.op("scalar", lambda e: e.activation(out=p_sb[u][:], in_=S5, func=AF.Exp, scale=0.125),
                     reads=[R_ps6, R_ps7], writes=[R_psb[u]])

                def fn(e):
                    last = None
                    for par in range(2):
                        last = e.tensor_tensor(out=p_sb[u][:, par, :, :, :], in0=p_sb[u][:, par, :, :, :],
                                               in1=mask2[:].unsqueeze(2).to_broadcast([128, 2, 2, 128]), op=ALU.mult)
                    return last
                P.op("vector", fn, reads=[R_psb[u]] + CONSTS, writes=[R_psb[u]])

        def a_pv(n, qd):
            sl = n % 2
            g = qd // 2
            u = qd % 2
            kbs = [1] if n == 0 else [0, 1]
            O4 = pk4[:, 0:260].rearrange("p (h d) -> p h d", d=65)

            def fn(e):
                last = None
                for j in range(2):
                    for par in range(2):
                        for jj, kb in enumerate(kbs):
                            ksl = n % 3 if kb == 1 else (n - 1) % 3
                            last = e.matmul(O4[:, 2 * j + par, :], lhsT=p_sb[u][:, par, kb, j, :], rhs=vaug[ksl][:, g, 0:65],
                                            start=(jj == 0), stop=(jj == len(kbs) - 1))
                return last
            P.op("tensor", fn, reads=[R_psb[u], R_vaug[n % 3], R_vaug[(n - 1) % 3]], writes=[R_pk4])
            P.op("vector", lambda e: e.tensor_tensor(out=small[:, 8:12], in0=O4[:, :, 64], in1=sinkexp[:, qd * 4:qd * 4 + 4], op=ALU.add),
                 reads=[R_pk4, res("sinkexp")], writes=[R_small[6]])
            P.op("vector", lambda e: e.reciprocal(out=small[:, 12:16], in_=small[:, 8:12]),
                 reads=[R_small[6]], writes=[R_small[7]])
            ogt = OGT[:, u, :].rearrange("p (h d) -> p h d", d=64)
            P.op("vector", lambda e: e.tensor_tensor(out=ogt, in0=O4[:, :, 0:64],
                                                     in1=small[:, 12:16].unsqueeze(2).to_broadcast([128, 4, 64]), op=ALU.mult),
                 reads=[R_pk4, R_small[7]], writes=[R_OGT[u]])
            P.op("gpsimd", lambda e: e.tensor_tensor(out=OA[sl][:, qd * 256:(qd + 1) * 256], in0=OGT[:, u, :],
                                                     in1=SG[sl][:, qd * 256:(qd + 1) * 256], op=ALU.mult),
                 reads=[R_OGT[u], R_SG[sl]], writes=[R_OA[sl]])

        def a_oT(i, n):
            transposes(OA[n % 2], [R_OA[n % 2]], 8, fTo, [R_fTo])

        def a_y(i):
            bg, Rbg = nextbig()
            Wt, Wr = wt("wa")
            tm_proj(bg, Rbg, fTo, slice(0, 128), [R_fTo], Wt, Wr, 0, 1024)
            P.op("vector", lambda e: e.scalar_tensor_tensor(out=ga[:, i, :], in0=TM[i % 2][:], scalar=1.0, in1=bg[:], op0=ALU.add, op1=ALU.mult),
                 reads=[R_TM[i % 2], Rbg], writes=[R_ga[i]])

        def a_stage1_all(i, n):
            bg, Rbg = a_qproj(i)
            a_rope_q(n, bg, Rbg)
            a_kv(i, n)
            a_gate(i)
            a_qkT(i, n)

        def b_ebenb(i):
            P.op("scalar", lambda e: e.activation(out=FS[0][:, 0:512], in_=b_all[:, i, :], func=AF.Exp),
                 reads=[R_b[i]], writes=[R_FS[0]])
            P.op("scalar", lambda e: e.activation(out=FS[0][:, 512:1024], in_=b_all[:, i, :], func=AF.Exp, scale=-1.0),
                 reads=[R_b[i]], writes=[R_FS[0]])

        def b_qk(i):
            tsel = slice(i * 128, (i + 1) * 128)
            bg, Rbg = nextbig()
            Wt, Wr = wt("qkb")
            tm_proj(bg, Rbg, hT, tsel, [R_hT[i]], Wt, Wr, 0, 1024)
            q_ = HS[i % 2]

            def fn(e):
                e.scalar_tensor_tensor(out=q_[:, 0:512], in0=bg[:, 0:512], scalar=128.0 ** -0.5, in1=FS[0][:, 0:512],
                                       op0=ALU.mult, op1=ALU.mult)
                return e.tensor_tensor(out=q_[:, 512:1024], in0=bg[:, 512:1024], in1=FS[0][:, 512:1024], op=ALU.mult)
            P.op("vector", fn, reads=[Rbg, R_FS[0]], writes=[R_HS[i % 2]])

        def b_v(i):
            tsel = slice(i * 128, (i + 1) * 128)
            bg, Rbg = nextbig()
            Wt, Wr = wt("vb")
            tm_proj(bg, Rbg, hT, tsel, [R_hT[i]], Wt, Wr, 0, 1024)
            P.op("scalar", lambda e: e.activation(out=HS[2 + i % 2][:], in_=bg[:], func=AF.Copy), reads=[Rbg], writes=[R_HS[2 + i % 2]])

        def b_qkT(i):
            transposes(HS[i % 2], [R_HS[i % 2]], 8, fTq[i % 2], [R_fTq[i % 2]])

        def b_gate(i):
            tsel = slice(i * 128, (i + 1) * 128)
            bg, Rbg = nextbig()
            Wt, Wr = wt("gb")
            tm_proj(bg, Rbg, hT, tsel, [R_hT[i]], Wt, Wr, 0, 1024)
            P.op("scalar", lambda e: e.activation(out=FS[0][:], in_=bg[:], func=AF.Tanh, scale=0.5),
                 reads=[Rbg], writes=[R_FS[0]])
            P.op("vector", lambda e: e.scalar_tensor_tensor(out=FS[0][:], in0=FS[0][:], scalar=1.0, in1=bg[:], op0=ALU.add, op1=ALU.mult),
                 reads=[R_FS[0], Rbg], writes=[R_FS[0]])
            P.op("gpsimd", lambda e: e.tensor_tensor(out=SG[i % 2][:].rearrange("p (h v) -> p h v", h=4), in0=FS[0][:].rearrange("p (h v) -> p h v", h=4),
                                                     in1=bnw_bc[:, 0, :].unsqueeze(1).to_broadcast([128, 4, 256]), op=ALU.mult),
                 reads=[R_FS[0], res("bnw")], writes=[R_SG[i % 2]])

        def b_att(i):
            att4 = pk4[:].rearrange("p (h i) -> p h i", h=4)
            fq = fTq[i % 2]

            def fn(e):
                last = None
                for h in range(4):
                    last = e.matmul(att4[:, h, :], lhsT=fq[:, 4 + h, :], rhs=fq[:, h, :], start=True, stop=True)
                return last
            P.op("tensor", fn, reads=[R_fTq[i % 2]], writes=[R_pk4])
            attm = kdup_att[:]
            P.op("vector", lambda e: e.tensor_tensor(out=attm, in0=att4, in1=mask2[:, 1, :].unsqueeze(1).to_broadcast([128, 4, 128]), op=ALU.mult),
                 reads=[R_pk4] + CONSTS, writes=[R_attm])

        def b_o(i, n):
            o4 = ps67[:].rearrange("p (h v) -> p h v", h=4)
            fq = fTq[i % 2]
            vs = HS[2 + i % 2]

            def fn(e):
                last = None
                for h in range(4):
                    last = e.matmul(o4[:, h, :], lhsT=kdup_att[:, h, :], rhs=vs[:, h * 256:(h + 1) * 256], start=True, stop=(n == 0))
                    if n > 0:
                        last = e.matmul(o4[:, h, :], lhsT=fq[:, h, :], rhs=S_b[:, h, :], start=False, stop=True)
                return last
            P.op("tensor", fn, reads=[R_attm, R_HS[2 + i % 2], R_fTq[i % 2], R_Sb], writes=[R_ps6, R_ps7])
            for h in range(4):
                P.op("scalar", (lambda h: lambda e: e.activation(out=OA[1][:, 0:256], in_=o4[:, h, :], func=AF.Square,
                                                                 accum_out=small[:, 16 + h:17 + h]))(h),
                     reads=[R_ps6, R_ps7], writes=[R_OA[1], R_small[0]])
            rstd_pool(small[:, 16:20], 4, 1.0 / 256.0, small[:, 20:24], small[:, 24:28], R_small[0], R_small[1], R_small[2])

        def b_inc(i, pair):
            qk_ = HS[i % 2]
            vs = HS[2 + i % 2]

            def fn(e):
                last = None
                for hh in range(2):
                    h = pair * 2 + hh
                    last = e.matmul(pk4[:, hh * 256:(hh + 1) * 256], lhsT=qk_[:, 512 + h * 128:512 + (h + 1) * 128],
                                    rhs=vs[:, h * 256:(h + 1) * 256], start=True, stop=True)
                return last
            P.op("tensor", fn, reads=[R_HS[i % 2], R_HS[2 + i % 2]], writes=[R_pk4])
            P.op("vector", lambda e: e.tensor_tensor(out=FS[2][:, pair * 512:(pair + 1) * 512],
                                                     in0=S_f[:, pair * 2:pair * 2 + 2, :].rearrange("p h v -> p (h v)"), in1=pk4[:], op=ALU.add),
                 reads=[R_Sf, R_pk4], writes=[R_FS[2]])

        def b_supd(i):
            def fn(e):
                dcb = decay[:, i * 4:(i + 1) * 4].unsqueeze(2).to_broadcast([128, 4, 256])
                e.tensor_tensor(out=S_f[:], in0=FS[2][:].rearrange("p (h v) -> p h v", h=4), in1=dcb, op=ALU.mult)
                return e.tensor_tensor(out=S_b[:], in0=FS[2][:].rearrange("p (h v) -> p h v", h=4), in1=dcb, op=ALU.mult)
            P.op("gpsimd", fn, reads=[R_FS[2], R_decay], writes=[R_Sf, R_Sb])

        def b_onorm(i):
            o4 = ps67[:].rearrange("p (h v) -> p h v", h=4)
            P.op("vector", lambda e: e.tensor_tensor(out=FS[1][:].rearrange("p (h v) -> p h v", h=4), in0=o4,
                                                     in1=small[:, 24:28].unsqueeze(2).to_broadcast([128, 4, 256]), op=ALU.mult),
                 reads=[R_ps6, R_ps7, R_small[2]], writes=[R_FS[1]])
            P.op("vector", lambda e: e.tensor_tensor(out=OA[0][:], in0=FS[1][:], in1=SG[i % 2][:], op=ALU.mult),
                 reads=[R_FS[1], R_SG[i % 2]], writes=[R_OA[0]])

        bm_state = {}

        def b_m_proj(i):
            tsel = slice(i * 128, (i + 1) * 128)
            bg, Rbg = nextbig()
            Wt, Wr = wt("mb")
            tm_proj(bg, Rbg, hT, tsel, [R_hT[i]], Wt, Wr, 0, 1024)
            bm_state["bg"] = (bg, Rbg)

        def b_m_tanh(i):
            bg, Rbg = bm_state["bg"]
            P.op("scalar", lambda e: e.activation(out=TM[0][:], in_=bg[:], func=AF.Tanh, scale=0.5),
                 reads=[Rbg], writes=[R_TM[0]])

        def b_yT(i):
            transposes(OA[0], [R_OA[0]], 8, fTo, [R_fTo])

        def b_y(i):
            bg, Rbg = nextbig()
            Wt, Wr = wt("wb")
            tm_proj(bg, Rbg, fTo, slice(0, 128), [R_fTo], Wt, Wr, 0, 1024)
            P.op("vector", lambda e: e.scalar_tensor_tensor(out=FS[1][:], in0=TM[0][:], scalar=1.0, in1=bg[:], op0=ALU.add, op1=ALU.mult),
                 reads=[R_TM[0], Rbg], writes=[R_FS[1]])
            P.op("gpsimd", lambda e: e.tensor_tensor(out=ga[:, i, :], in0=FS[1][:], in1=ga[:, i, :], op=ALU.add),
                 reads=[R_FS[1], R_ga[i]], writes=[R_ga[i]])

        R_p1b = [Res("p1b0"), Res("p1b1"), Res("p1b2")]

        def p1b_load(H1, i):
            n = H1 * BPH + i
            P.op("sync", lambda e: e.dma_start(out=FS[3][:], in_=x[n * 128:(n + 1) * 128, :]),
                 writes=[R_FS[3]], chan="fsx3")

        def p1b_front(H1, i):
            P.op("scalar", lambda e: e.activation(out=OA[1][:], in_=FS[3][:], func=AF.Square, accum_out=small[:, 50:51]),
                 reads=[R_FS[3]], writes=[R_OA[1], R_p1b[0]])
            rstd_pool(small[:, 50:51], 1, 1.0 / D, small[:, 51:52], small[:, 52:53], R_p1b[0], R_p1b[1], R_p1b[2])
            P.op("vector", lambda e: e.scalar_tensor_tensor(out=TM[1][:], in0=FS[3][:], scalar=small[:, 52:53],
                                                            in1=normw_bc[:, 0, :], op0=ALU.mult, op1=ALU.mult),
                 reads=[R_FS[3], R_p1b[2], res("normw")], writes=[R_TM[1]])
            if i + 1 < BPH:
                p1b_load(H1, i + 1)

        def p1b_back(H1, i):
            transposes(TM[1], [R_TM[1]], 8, hT, [R_hT[i]], dst_sel=slice(i * 128, (i + 1) * 128))

        cT_done = set()

        def c_T(H, i):
            if (H, i) in cT_done:
                return
            cT_done.add((H, i))
            n = H * BPH + i
            b2 = i % 2
            xs = i % 3
            P.op("sync", lambda e: e.dma_start(out=FS[xs][:], in_=x[n * 128:(n + 1) * 128, :]),
                 writes=[R_FS[xs]], chan=f"fsx{xs}")
            transposes(ga[:, i, :], [R_ga[i]], 8, fTq[b2], [R_fTq[b2]])

        def c_P(H, i):
            b2 = i % 2
            Wt, Wr = wt("wo")
            tm_proj(bigs[b2], R_bigs[b2], fTq[b2], slice(0, 128), [R_fTq[b2]], Wt, Wr, 0, 1024)

        def c_tail(H, i):
            n = H * BPH + i
            b2 = i % 2
            s = i % 3
            P.op("vector", lambda e: e.scalar_tensor_tensor(out=FS[s][:], in0=bigs[b2][:], scalar=0.25, in1=FS[s][:], op0=ALU.mult, op1=ALU.add),
                 reads=[R_bigs[b2], R_FS[s]], writes=[R_FS[s]])
            P.op("scalar", lambda e: e.activation(out=OA[1][:], in_=FS[s][:], func=AF.Square, accum_out=small[:, 32 + b2:33 + b2]),
                 reads=[R_FS[s]], writes=[R_OA[1], R_small[b2]])
            P.op("scalar", lambda e: e.activation(out=small[:, 34 + b2:35 + b2], in_=small[:, 32 + b2:33 + b2], func=AF.Ln, scale=1.0 / D, bias=epsb[:, 0:1]),
                 reads=[R_small[b2], res("epsb")], writes=[R_small[2 + b2]])
            P.op("scalar", lambda e: e.activation(out=small[:, 36 + b2:37 + b2], in_=small[:, 34 + b2:35 + b2], func=AF.Exp, scale=-0.5),
                 reads=[R_small[2 + b2]], writes=[R_small[4 + b2]])
            P.op("vector", lambda e: e.scalar_tensor_tensor(out=FS[s][:], in0=FS[s][:], scalar=small[:, 36 + b2:37 + b2], in1=fnw_bc[:, 0, :],
                                                            op0=ALU.mult, op1=ALU.mult),
                 reads=[R_FS[s], R_small[4 + b2], res("fnw")], writes=[R_FS[s]])
            P.op("sync", lambda e: e.dma_start(out=out[n * 128:(n + 1) * 128, :], in_=FS[s][:]),
                 reads=[R_FS[s]], writes=[R_FS[s]], chan=f"fsx{s}")

        epsb = sb("epsb", [128, 1], F32)
        P.op("gpsimd", lambda e: e.memset(epsb[:], EPS), writes=[res("epsb")])
        kdup_att = sb("attm", [128, 4, 128], BF16)
        R_attm = Res("attm")

        p2_early = set()

        def p2_blT(t, pst, Rpst):
            if t == 0:
                P.op("gpsimd", lambda e: e.memset(FS[3][0:32, :], 1.0), writes=[R_FS[3]])

            def fn(e):
                last = None
                for c in range(8):
                    last = e.matmul(pst[0:16, 0:512], lhsT=wlow[:, c, :], rhs=hT[:, c, t * 512:(t + 1) * 512],
                                    start=(c == 0), stop=(c == 7))
                return last
            P.op("tensor", fn, reads=R_hT[t * 4:(t + 1) * 4] + [res("wlow")], writes=[Rpst])
            P.op("vector", lambda e: e.tensor_copy(out=blT[0:16, t * 512:(t + 1) * 512], in_=pst[0:16, 0:512]),
                 reads=[Rpst], writes=[R_blT])

        def emit_phase2(hosted=None):
            alt = hosted is not None
            if alt:
                flb = [OGT[:].rearrange("p u d -> p (u d)"),
                       p_sb[0][:].rearrange("p a b c d -> p (a b c d)").bitcast(F32)]
                Rflb = [[R_OGT[0], R_OGT[1]], [R_psb[0]]]
            else:
                flb = [FS[2][:, 0:512], FS[2][:, 512:1024]]
                Rflb = [[res("fl0"), R_FS[2]], [res("fl1"), R_FS[2]]]
            for t in range(BPH * 128 // 512):
                if t not in p2_early:
                    p2_blT(t, pk4, R_pk4)
            p2_early.clear()

            def p2_x(i):
                pg = ps6 if i % 2 == 0 else ps7
                Rpg = R_ps6 if i % 2 == 0 else R_ps7
                fl = flb[i % 2]
                Rw = Rflb[i % 2]
                Rr = Rw[:1] if not alt else Rw
                P.op("tensor", lambda e: e.matmul(pg[:, 0:512], lhsT=blT[0:17, i * 128:(i + 1) * 128], rhs=gup_sb[0:17, :], start=True, stop=True),
                     reads=[R_blT, res("gup")], writes=[Rpg])
                P.op("scalar", lambda e: e.activation(out=fl, in_=pg[:, 0:512], func=AF.Exp, scale=-1.0),
                     reads=[Rpg], writes=Rw)
                P.op("scalar", lambda e: e.activation(out=fl, in_=fl, func=AF.Ln, bias=1.0),
                     reads=Rr, writes=Rw)

            def p2_y(i):
                pg = ps6 if i % 2 == 0 else ps7
                Rpg = R_ps6 if i % 2 == 0 else R_ps7
                fl = flb[i % 2]
                Rw = Rflb[i % 2]
                Rr = Rw[:1] if not alt else Rw
                P.op("tensor", lambda e: e.matmul(pg[:, 0:512], lhsT=triL[:], rhs=fl, start=True, stop=True),
                     reads=Rr + CONSTS, writes=[Rpg])
                P.op("scalar", lambda e: e.activation(out=b_all[:, i, :], in_=pg[:, 0:512], func=AF.Copy),
                     reads=[Rpg], writes=[R_b[i]])

                def fn(e):
                    last = None
                    for h in range(4):
                        cc = (i * 4 + h) * 2
                        last = e.matmul(pk4[:, cc:cc + 2], lhsT=fl[:, h * 128:(h + 1) * 128], rhs=triL[:, 126:128],
                                        start=True, stop=True)
                    return last
                P.op("tensor", fn, reads=Rr + CONSTS, writes=[R_pk4])

            if not alt:
                for k_ in range(2):
                    r_ = res(f"fl{k_}")
                    r_.w = R_FS[2].w
                    r_.r = dict(R_FS[2].r)
            else:
                c_T(hosted, 0)
                c_T(hosted, 1)
            p2_x(0)
            for i in range(BPH):
                if alt:
                    c_P(hosted, i)
                if i + 1 < BPH:
                    p2_x(i + 1)
                if alt and i + 2 < BPH:
                    c_T(hosted, i + 2)
                p2_y(i)
                if alt:
                    c_tail(hosted, i)
            if not alt:
                for k_ in range(2):
                    for c_, v_ in res(f"fl{k_}").r.items():
                        R_FS[2].r[c_] = max(R_FS[2].r.get(c_, 0), v_)
            P.op("scalar", lambda e: e.activation(out=decay[:], in_=pk4[:, 0:BPH * 8].rearrange("p (m two) -> p m two", two=2)[:, :, 1],
                                                  func=AF.Exp),
                 reads=[R_pk4], writes=[R_decay])


        done = False
        for H in range(NHALF):
            blk0 = H * BPH
            if H == 0:
                wload("q", w_in_v, O_AQ)
                load_group(WKV, R_WKV, w_in_v, O_AK, 256, "wkv")
                wload("ga", w_in_v, O_AG)
                wload("ma", w_in_v, O_MA)
                wload("wa", w_a_v, 0)
                p1_head(H, 0)
                p1_head(H, 1)
                for i in range(BPH):
                    if i + 2 < BPH:
                        p1_head(H, i + 2)
                    p1_tail(H, i)

            if H == 0:
                emit_rope_tables()
            if H == 0:
                bg0, Rbg0 = a_qproj(0)
                a_rope_q(0, bg0, Rbg0)
                a_kv(0, 0)
                a_qkT(0, 0)
                emit_phase2()
            if dbg == "p2":
                dump_and_finish(); done = True; break

            if H != 0:
                a_stage1_all(0, blk0)
            for i in range(BPH):
                n = blk0 + i
                nxt = i + 1 < BPH
                prv = i - 1 >= 0
                last = not nxt
                a_scores(n, 0)
                if H == 0 and i == 0:
                    a_gate(0)
                if nxt:
                    bg, Rbg = a_qproj(i + 1)
                    a_rope_q(n + 1, bg, Rbg)
                    if i == BPH - 2:
                        wfree("q")
                        wload("vb", w_in_v, O_BV)
                if last:
                    if H + 1 < NHALF:
                        p1b_load(H + 1, 0)
                    b_ebenb(0)
                    b_qk(0)
                if prv:
                    a_oT(i - 1, n - 1)
                a_pv(n, 0)
                a_scores(n, 1)
                if nxt:
                    a_kv(i + 1, n + 1)
                if prv:
                    a_y(i - 1)
                a_pv(n, 1)
                a_scores(n, 2)
                if nxt:
                    a_qT(i + 1, n + 1)
                    a_gate(i + 1)
                    if i == BPH - 2:
                        wfree("ga")
                        wload("gb", w_in_v, O_BG)
                    a_kT(i + 1, n + 1)
                if last:
                    b_qkT(0)
                a_pv(n, 2)
                a_scores(n, 3)
                a_mproj(i)
                if last:
                    wfree("ma")
                    wload("mb", w_in_v, O_MB)
                    b_v(0)
                a_pv(n, 3)
                if H == 0 and i in (2, 5):
                    flush_stores()
                if i == 1:
                    wload("qkb", w_in_v, O_BQ)
            a_oT(BPH - 1, blk0 + BPH - 1)
            b_gate(0)
            a_y(BPH - 1)
            wfree("wa")
            wload("wb", w_b_v, 0)
            if dbg == "A":
                dump_and_finish(); done = True; break

            import os
            BO = os.environ.get("B_ORDER", "")
            for i in range(BPH):
                n = blk0 + i
                nxt = i + 1 < BPH
                b_att(i)
                if nxt and "1" not in BO:
                    b_ebenb(i + 1)
                    b_qk(i + 1)
                    if i == BPH - 2:
                        wfree("qkb")
                        wload("wo", w_o_v, 0)
                b_o(i, n)
                if not nxt:
                    b_m_proj(i)
                if n < NBLK - 1:
                    b_inc(i, 0)
                if not nxt:
                    c_T(H, 0)
                if nxt and "2" not in BO:
                    b_v(i + 1)
                    if i == BPH - 2:
                        wfree("vb")
                        if H + 1 < NHALF:
                            wload("q", w_in_v, O_AQ)
                if n < NBLK - 1:
                    b_inc(i, 1)
                    b_supd(i)
                b_onorm(i)
                if H + 1 < NHALF:
                    p1b_front(H + 1, i)
                    if not nxt:
                        bgf, Rbgf = nextbig()
                        p2_blT(0, bgf, Rbgf)
                        p2_early.add(0)
                if nxt and "3" not in BO:
                    b_qkT(i + 1)
                if nxt and "4" not in BO:
                    b_gate(i + 1)
                    if i == BPH - 2:
                        wfree("gb")
                        if H + 1 < NHALF:
                            wload("ga", w_in_v, O_AG)
                b_yT(i)
                if nxt:
                    b_m_proj(i)
                b_m_tanh(i)
                b_y(i)
                if H + 1 < NHALF:
                    p1b_back(H + 1, i)
                if H == 0 and i == 2:
                    flush_stores()
                if nxt and "1" in BO:
                    b_ebenb(i + 1)
                    b_qk(i + 1)
                if nxt and "2" in BO:
                    b_v(i + 1)
                if nxt and "3" in BO:
                    b_qkT(i + 1)
                if nxt and "4" in BO:
                    b_gate(i + 1)
            wfree("mb"); wfree("wb")
            if dbg == "B":
                dump_and_finish(); done = True; break

            if H + 1 < NHALF:
                wload("ma", w_in_v, O_MA)
                wload("wa", w_a_v, 0)
                emit_phase2(hosted=H)
                flush_stores()
            else:
                c_T(H, 0)
                c_T(H, 1)
                for i in range(BPH):
                    c_P(H, i)
                    if i + 2 < BPH:
                        c_T(H, i + 2)
                    c_tail(H, i)
            wfree("wo")
            if dbg == "C":
                dump_and_finish(); done = True; break

        if not done:
            P.final_wait("sync", ["fsx0", "fsx1", "fsx2", "fsx3"])
            P.emit()
    return nc


_NC_CACHE = {}


def _get_nc():
    if "nc" not in _NC_CACHE:
        _NC_CACHE["nc"] = build()
    return _NC_CACHE["nc"]


def _in_maps(x, positions, norm_w, w_in, a_sinks, b_gate_up, b_gate_bias, b_out_norm_w,
             w_a_proj, w_b_proj, w_out, final_norm_w):
    f32 = np.float32
    invf = (10000.0 ** (-np.arange(32, dtype=np.float32) / np.float32(32))).astype(f32)
    invf_b = np.ascontiguousarray(np.broadcast_to(invf[None, :], (128, 32)))
    gup = np.ascontiguousarray(np.concatenate([np.asarray(b_gate_up[0], f32), np.asarray(b_gate_bias[0], f32)[None, :]], axis=0))
    shared = {
        "invf": invf_b,
        "norm_w": np.ascontiguousarray(np.asarray(norm_w, f32).reshape(1, D)),
        "fnorm_w": np.ascontiguousarray(np.asarray(final_norm_w, f32).reshape(1, D)),
        "bnw": np.ascontiguousarray(np.asarray(b_out_norm_w, f32).reshape(1, 256)),
        "sinks": np.ascontiguousarray(np.asarray(a_sinks, f32).reshape(1, 16)),
        "gup": gup,
        "w_in": np.ascontiguousarray(np.asarray(w_in[0], f32)),
        "w_a": np.ascontiguousarray(np.asarray(w_a_proj[0], f32)),
        "w_b": np.ascontiguousarray(np.asarray(w_b_proj[0], f32)),
        "w_o": np.ascontiguousarray(np.asarray(w_out[0], f32)),
    }
    maps = []
    for b in range(8):
        m = dict(shared)
        m["x"] = np.ascontiguousarray(np.asarray(x[b], f32))
        m["pos"] = np.ascontiguousarray(np.asarray(positions[b], np.int32).reshape(NBLK, 128).T)
        maps.append(m)
    return maps


def kernel(x, positions, norm_w, w_in, a_sinks, b_gate_up, b_gate_bias, b_out_norm_w,
           w_a_proj, w_b_proj, w_out, final_norm_w):
    nc = _get_nc()
    maps = _in_maps(x, positions, norm_w, w_in, a_sinks, b_gate_up, b_gate_bias, b_out_norm_w,
                    w_a_proj, w_b_proj, w_out, final_norm_w)
    res_ = run_bass_kernel_spmd(nc, maps, core_ids=list(range(8)))
    return np.stack([np.asarray(r["out"], np.float32) for r in res_.results], axis=0)
```

```python
import contextlib
import math

import numpy as np
import concourse.bass as bass
import concourse.mybir as mybir
from concourse.bass_utils import run_bass_kernel_spmd

F32 = mybir.dt.float32
BF16 = mybir.dt.bfloat16
I32 = mybir.dt.int32
AF = mybir.ActivationFunctionType
ALU = mybir.AluOpType

T = 2048
D = 1024
NBLK = 16
NHALF = 2
BPH = NBLK // NHALF
WCH = 4
EPS = 1e-5
O_AQ, O_AK, O_AV, O_AG = 0, 1024, 1152, 1280
O_BQ, O_BK, O_BV, O_BG = 2304, 2816, 3328, 4352
O_LOW, O_MA, O_MB = 5376, 5392, 6416
IN_W = 7440


class Res:
    __slots__ = ("name", "w", "r")

    def __init__(self, name):
        self.name = name
        self.w = None
        self.r = {}


class Prog:
    ENGS = ("sync", "scalar", "vector", "gpsimd", "tensor")

    def __init__(self, nc):
        self.nc = nc
        self.lists = {e: [] for e in self.ENGS}
        self.sems = {}
        self.count = {}
        self.waited = {e: {} for e in self.ENGS}
        self.sem_ctx = []

    def _sem(self, chan):
        if chan not in self.sems:
            cm = self.nc.semaphore("s_" + chan)
            s = cm.__enter__()
            self.sem_ctx.append(cm)
            self.sems[chan] = s
            self.count[chan] = 0
        return self.sems[chan]

    def op(self, eng, fn, reads=(), writes=(), chan=None):
        if chan is None:
            chan = eng
        step = 1 if chan == eng else 16
        need = {}
        for r in reads:
            if r.w is not None:
                c, v = r.w
                need[c] = max(need.get(c, 0), v)
        for w in writes:
            if w.w is not None:
                c, v = w.w
                need[c] = max(need.get(c, 0), v)
            for c, v in w.r.items():
                need[c] = max(need.get(c, 0), v)
        waits = []
        for c, v in need.items():
            if c == eng and eng == "tensor":
                continue
            if self.waited[eng].get(c, 0) < v:
                self.waited[eng][c] = v
                waits.append((self.sems[c], v))
        sem = self._sem(chan)
        self.count[chan] += step
        val = self.count[chan]
        self.lists[eng].append((waits, fn, sem, step))
        for r in reads:
            r.r[chan] = max(r.r.get(chan, 0), val)
        for w in writes:
            w.w = (chan, val)
            w.r = {}
        return val

    def final_wait(self, eng, chans):
        waits = [(self.sems[c], self.count[c]) for c in chans if c in self.sems]
        self.lists[eng].append((waits, None, None, None))

    def emit(self):
        with self.nc.Block() as block:
            def mk(ename):
                def body(e):
                    for waits, fn, sem, step in self.lists[ename]:
                        for s, v in waits:
                            e.wait_ge(s, v)
                        if fn is not None:
                            fn(e).then_inc(sem, step)
                return body
            for ename in self.ENGS:
                if self.lists[ename]:
                    getattr(block, ename)(mk(ename))
        for cm in reversed(self.sem_ctx):
            cm.__exit__(None, None, None)


def build(dbg=None):
    nc = bass.Bass("TRN2", target_bir_lowering=False)

    def din(name, shape, dt=F32):
        return nc.dram_tensor(name, shape, dt, kind="ExternalInput").ap()

    x = din("x", [T, D])
    pos = din("pos", [128, NBLK], I32)
    invf = din("invf", [128, 32])
    norm_w = din("norm_w", [1, D])
    fnorm_w = din("fnorm_w", [1, D])
    bnw = din("bnw", [1, 256])
    sinks = din("sinks", [1, 16])
    gup = din("gup", [17, 512])
    w_in = din("w_in", [D, IN_W])
    w_a = din("w_a", [D, D])
    w_b = din("w_b", [D, D])
    w_o = din("w_o", [D, D])
    out = nc.dram_tensor("out", [T, D], F32, kind="ExternalOutput").ap()
    WNAMES = ("q", "ga", "ma", "wa", "qkb", "vb", "gb", "mb", "wb", "wo")
    wscr = {nm: nc.dram_tensor("wscr_" + nm, [128, 8, D], BF16, kind="Internal").ap() for nm in WNAMES}
    dbg_out = {}
    if dbg:
        for nm, shp in (("d_hT", [128, 8, 1024]), ("d_b", [128, 8, 512]), ("d_ga", [128, 8, 1024]),
                        ("d_dec", [128, 32]), ("d_cos", [128, 16, 32]), ("d_sin", [128, 16, 32])):
            dbg_out[nm] = nc.dram_tensor(nm, shp, F32, kind="ExternalOutput").ap()

    w_in_v = w_in.rearrange("(c p) n -> p c n", p=128)
    w_a_v = w_a.rearrange("(c p) n -> p c n", p=128)
    w_b_v = w_b.rearrange("(c p) n -> p c n", p=128)
    w_o_v = w_o.rearrange("(c p) n -> p c n", p=128)

    with contextlib.ExitStack() as es:
        def sb(name, shape, dt):
            return es.enter_context(nc.sbuf_tensor(name, shape, dt))

        def ps(name, shape, dt):
            return es.enter_context(nc.psum_tensor(name, shape, dt))

        P = Prog(nc)
        R = {}

        def res(name):
            if name not in R:
                R[name] = Res(name)
            return R[name]

        ident = sb("ident", [128, 128], BF16)
        triL = sb("triL", [128, 128], F32)
        mask2 = sb("mask2", [128, 2, 128], BF16)
        neghalf = sb("neghalf", [128, 8], F32)
        normw_bc = sb("normw_bc", [128, 1, D], F32)
        fnw_bc = sb("fnw_bc", [128, 1, D], F32)
        bnw_bc = sb("bnw_bc", [128, 1, 256], F32)
        sink_bc = sb("sink_bc", [128, 1, 16], F32)
        sinkexp = sb("sinkexp", [128, 16], F32)
        invf_sb = sb("invf_sb", [128, 32], F32)
        pos_i = sb("pos_i", [128, NBLK], I32)
        pos_f = sb("pos_f", [128, NBLK], F32)
        cos_t = sb("cos_t", [128, NBLK, 32], F32)
        sin_t = sb("sin_t", [128, NBLK, 32], F32)
        gup_sb = sb("gup_sb", [17, 512], F32)
        wlow = sb("wlow", [128, 8, 16], BF16)
        decay = sb("decay", [128, BPH * 4], F32)

        hT = sb("hT", [128, 8, BPH * 128], BF16)
        b_all = sb("b_all", [128, BPH, 512], F32)
        ga = sb("ga", [128, BPH, D], BF16)
        S_f = sb("S_f", [128, 4, 256], F32)
        S_b = sb("S_b", [128, 4, 256], BF16)
        kT2 = [sb(f"kT2_{i}", [128, 2, 128], BF16) for i in range(3)]
        vaug = [sb(f"vaug_{i}", [128, 2, 66], BF16) for i in range(3)]

        WS = [sb(f"ws{i}", [128, 8, D], BF16) for i in range(5)]
        WKV = sb("wkv", [128, 8, 256], BF16)
        R_WS = [[Res(f"ws{i}_{c}") for c in range(8)] for i in range(5)]
        R_WKV = [Res(f"wkv_{c}") for c in range(8)]

        FS = [sb(f"fs{i}", [128, D], F32) for i in range(4)]
        HS = [sb(f"hs{i}", [128, D], BF16) for i in range(4)]
        SG = [sb(f"sg{i}", [128, D], BF16) for i in range(2)]
        TM = [sb(f"tm{i}", [128, D], BF16) for i in range(2)]
        OA = [sb(f"oa{i}", [128, D], BF16) for i in range(2)]
        OGT = sb("ogt", [128, 2, 256], F32)
        fTq = [sb(f"fTq{i}", [128, 8, 128], BF16) for i in range(2)]
        fTo = sb("fTo", [128, 8, 128], BF16)
        p_sb = [sb(f"p_sb{i}", [128, 2, 2, 2, 128], BF16) for i in range(2)]
        kdup = sb("kdup", [128, 2, 2, 64], BF16)
        small = sb("small", [128, 64], F32)
        R_small = [Res(f"small{i}") for i in range(10)]
        blT = FS[3][0:17, :]

        bigs = [ps("bigA", [128, D], F32), ps("bigB", [128, D], F32)]
        R_bigs = [Res("bigA"), Res("bigB")]
        pk4 = ps("pk4", [128, 512], F32)
        tp5f = ps("tp5", [128, 512], F32)
        tp5 = tp5f[:].bitcast(BF16).rearrange("p (c t) -> p c t", t=128)
        ps67 = ps("ps67", [128, 1024], F32)
        ps6 = ps67[:, 0:512]
        ps7 = ps67[:, 512:1024]
        R_pk4, R_tp5, R_ps6, R_ps7 = [Res(n) for n in "pk4 tp5 ps6 ps7".split()]
        bigtog = [0]

        def nextbig():
            k = bigtog[0]
            bigtog[0] ^= 1
            return bigs[k], R_bigs[k]

        R_FS = [Res("fs0"), Res("fs1"), Res("fs2"), Res("fs3")]
        R_HS = [Res(f"hs{i}") for i in range(4)]
        R_SG = [Res("sg0"), Res("sg1")]
        R_TM = [Res("tm0"), Res("tm1")]
        R_OA = [Res("oa0"), Res("oa1")]
        R_OGT = [Res("ogt0"), Res("ogt1")]
        R_fTq = [Res("fTq0"), Res("fTq1")]
        R_fTo = Res("fTo")
        R_psb = [Res("psb0"), Res("psb1")]
        R_kdup = Res("kdup")
        R_kT2 = [Res("kT2_0"), Res("kT2_1"), Res("kT2_2")]
        R_vaug = [res("vaug0"), res("vaug1"), res("vaug2")]
        R_hT = [Res(f"hT{i}") for i in range(BPH)]
        R_b = [Res(f"b{i}") for i in range(BPH)]
        R_ga = [Res(f"ga{i}") for i in range(BPH)]
        R_blT = R_FS[3]
        R_decay = Res("decay")
        R_Sf = res("S_f")
        R_Sb = res("S_b")

        const_res = [res(n) for n in ("normw", "fnw", "bnw", "sink", "invf", "pos", "gup")]
        for b_ in range(2):
            P.op("sync", (lambda b_: lambda e: e.dma_start(out=FS[b_][:], in_=x[b_ * 128:(b_ + 1) * 128, :]))(b_),
                 writes=[R_FS[b_]], chan=f"fsx{b_}")
        P.op("sync", lambda e: e.dma_start(out=pos_i[:], in_=pos), chan="const")
        P.op("sync", lambda e: e.dma_start(out=invf_sb[:], in_=invf), chan="const")
        P.op("sync", lambda e: e.dma_start(out=normw_bc[:], in_=norm_w.partition_broadcast(128)), chan="const")
        P.op("sync", lambda e: e.dma_start(out=sink_bc[:], in_=sinks.partition_broadcast(128)), chan="const")
        P.op("sync", lambda e: e.dma_start(out=gup_sb[:], in_=gup), chan="const")
        P.op("sync", lambda e: e.dma_start(out=bnw_bc[:], in_=bnw.partition_broadcast(128)), chan="const")
        P.op("sync", lambda e: e.dma_start(out=fnw_bc[:], in_=fnorm_w.partition_broadcast(128)), chan="const")
        for r_ in const_res:
            r_.w = ("const", P.count["const"])
        P.op("gpsimd", lambda e: e.dma_start(out=wlow[:], in_=w_in_v[:, :, O_LOW:O_LOW + 16]),
             writes=[res("wlow")], chan="wlow")

        st = HS[0][:].bitcast(F32)
        onesf, negs, maskc_f, maskp_f = st[:, 0:128], st[:, 128:256], st[:, 256:384], st[:, 384:512]
        identf = HS[1][:].bitcast(F32)[:, 0:128]

        def setup_pool0(e):
            e.memset(onesf, 1.0)
            e.memset(negs, -1.0 / 16.0)
            e.memset(neghalf[:], -0.5)
            e.memset(S_f[:], 0.0)
            e.memset(S_b[:], 0.0)
            e.memset(vaug[0][:], 1.0)
            e.memset(vaug[1][:], 1.0)
            return e.memset(vaug[2][:], 1.0)
        P.op("gpsimd", setup_pool0, writes=[res("cm"), res("S_f"), res("S_b"), res("vaug0"), res("vaug1"), res("vaug2"), R_HS[0]])

        def setup_pool(e):
            e.affine_select(out=identf, in_=onesf, pattern=[[1, 128]], compare_op=ALU.is_equal,
                            fill=0.0, base=0, channel_multiplier=-1)
            e.affine_select(out=maskc_f, in_=onesf, pattern=[[1, 128]], compare_op=ALU.is_ge,
                            fill=0.0, base=0, channel_multiplier=-1)
            e.affine_select(out=maskp_f, in_=onesf, pattern=[[-1, 128]], compare_op=ALU.is_gt,
                            fill=0.0, base=0, channel_multiplier=1)
            return e.affine_select(out=triL[:], in_=negs, pattern=[[1, 128]], compare_op=ALU.is_ge,
                                   fill=0.0, base=0, channel_multiplier=-1)
        P.op("gpsimd", setup_pool, reads=[res("cm"), R_HS[0]], writes=[res("c0"), R_HS[0], R_HS[1]])

        def setup_pool2(e):
            e.tensor_copy(out=ident[:], in_=identf)
            e.tensor_copy(out=mask2[:, 0, :], in_=maskp_f)
            return e.tensor_copy(out=mask2[:, 1, :], in_=maskc_f)
        P.op("gpsimd", setup_pool2, reads=[res("c0"), R_HS[0], R_HS[1]], writes=[res("c1"), R_HS[0], R_HS[1]])
        CONSTS = [res("c0"), res("c1"), res("cm")]

        ROPE = [res("sin"), res("cos")]

        def emit_rope_tables():
            TWO_PI = 2.0 * math.pi
            C1 = 6.28125
            C2 = TWO_PI - C1
            PI_LO = 3.1415925
            ang = FS[0][:, 0:512].rearrange("p (n j) -> p n j", j=32)
            uu = FS[0][:, 512:1024].rearrange("p (n j) -> p n j", j=32)
            ki = FS[1][:, 0:512].bitcast(I32).rearrange("p (n j) -> p n j", j=32)
            kf = FS[1][:, 512:1024].rearrange("p (n j) -> p n j", j=32)
            rr = FS[2][:, 0:512].rearrange("p (n j) -> p n j", j=32)
            rc = FS[2][:, 512:1024].rearrange("p (n j) -> p n j", j=32)
            P.op("vector", lambda e: e.tensor_copy(out=pos_f[:], in_=pos_i[:]), reads=[res("pos")], writes=[res("pos_f")])
            P.op("vector", lambda e: e.tensor_tensor(out=ang, in0=pos_f[:].unsqueeze(2).to_broadcast([128, NBLK, 32]),
                                                     in1=invf_sb[:].unsqueeze(1).to_broadcast([128, NBLK, 32]), op=ALU.mult),
                 reads=[res("pos_f"), res("invf")], writes=[R_FS[0]])
            P.op("vector", lambda e: e.tensor_scalar(out=uu, in0=ang, scalar1=1.0 / TWO_PI, scalar2=None, op0=ALU.mult),
                 reads=[R_FS[0]], writes=[R_FS[0]])
            P.op("vector", lambda e: e.tensor_copy(out=ki, in_=uu), reads=[R_FS[0]], writes=[R_FS[1]])
            P.op("vector", lambda e: e.tensor_copy(out=kf, in_=ki), reads=[R_FS[1]], writes=[R_FS[1]])
            P.op("vector", lambda e: e.scalar_tensor_tensor(out=rr, in0=kf, scalar=-C1, in1=ang, op0=ALU.mult, op1=ALU.add),
                 reads=[R_FS[1], R_FS[0]], writes=[R_FS[2]])
            P.op("vector", lambda e: e.scalar_tensor_tensor(out=rr, in0=kf, scalar=-C2, in1=rr, op0=ALU.mult, op1=ALU.add),
                 reads=[R_FS[1], R_FS[2]], writes=[R_FS[2]])
            P.op("vector", lambda e: e.tensor_scalar(out=rc, in0=rr, scalar1=math.pi / 2, scalar2=None, op0=ALU.add),
                 reads=[R_FS[2]], writes=[R_FS[2]])
            P.op("vector", lambda e: e.tensor_single_scalar(out=uu, in_=rc, scalar=math.pi, op=ALU.is_gt),
                 reads=[R_FS[2]], writes=[R_FS[0]])
            P.op("vector", lambda e: e.scalar_tensor_tensor(out=rc, in0=uu, scalar=-TWO_PI, in1=rc, op0=ALU.mult, op1=ALU.add),
                 reads=[R_FS[0], R_FS[2]], writes=[R_FS[2]])
            P.op("vector", lambda e: e.tensor_scalar(out=FS[2][:], in0=FS[2][:], scalar1=-PI_LO, scalar2=PI_LO, op0=ALU.max, op1=ALU.min),
                 reads=[R_FS[2]], writes=[R_FS[2]])
            P.op("scalar", lambda e: e.activation(out=sin_t[:], in_=rr, func=AF.Sin), reads=[R_FS[2]], writes=[res("sin")])
            P.op("scalar", lambda e: e.activation(out=cos_t[:], in_=rc, func=AF.Sin), reads=[R_FS[2]], writes=[res("cos")])
            P.op("scalar", lambda e: e.activation(out=sinkexp[:], in_=sink_bc[:, 0, :], func=AF.Exp),
                 reads=[res("sink")], writes=[res("sinkexp")])


        free_slots = [0, 1, 2, 3, 4]
        W = {}

        def load_group(tile, rlist, src_view, col0, ncols, chan, extra_reads=()):
            for c0 in range(0, 8, WCH):
                P.op("gpsimd",
                     (lambda c0: lambda e: e.dma_start(out=tile[:, c0:c0 + WCH, 0:ncols], in_=src_view[:, c0:c0 + WCH, col0:col0 + ncols]))(c0),
                     reads=list(extra_reads), writes=rlist[c0:c0 + WCH], chan=chan)
            for c in range(8):
                rlist[c].w = (chan, P.count[chan])

        R_scr = {nm: Res("scr_" + nm) for nm in WNAMES}
        pending_store = []
        stored = set()

        def wload(name, src_view, col0, extra_reads=()):
            k = free_slots.pop(0)
            if name in stored:
                for c0 in range(0, 8, WCH):
                    P.op("sync",
                         (lambda c0: lambda e: e.dma_start(out=WS[k][:, c0:c0 + WCH, :], in_=wscr[name][:, c0:c0 + WCH, :]))(c0),
                         reads=[R_scr[name]], writes=R_WS[k][c0:c0 + WCH], chan=f"wh{k}")
                for c in range(8):
                    R_WS[k][c].w = (f"wh{k}", P.count[f"wh{k}"])
            else:
                load_group(WS[k], R_WS[k], src_view, col0, 1024, f"ws{k}", extra_reads)
                pending_store.append(name)
            W[name] = k

        def flush_stores():
            while pending_store:
                name = pending_store.pop(0)
                k = W[name]
                P.op("sync", (lambda name, k: lambda e: e.dma_start(out=wscr[name], in_=WS[k][:]))(name, k),
                     reads=R_WS[k], writes=[R_scr[name]], chan="scr_" + name)
                stored.add(name)

        def wfree(name):
            free_slots.append(W.pop(name))

        def wt(name):
            return WS[W[name]], R_WS[W[name]]

        def tm_proj(ps_tile, ps_res, lhs, lhs_sel, lhs_res, Wt, W_res, wcol0, ncols):
            def fn(e):
                last = None
                for j0 in range(0, ncols, 512):
                    w_ = min(512, ncols - j0)
                    for c in range(8):
                        last = e.matmul(ps_tile[:, j0:j0 + w_], lhsT=lhs[:, c, lhs_sel],
                                        rhs=Wt[:, c, wcol0 + j0:wcol0 + j0 + w_], start=(c == 0), stop=(c == 7))
                return last
            P.op("tensor", fn, reads=list(lhs_res) + list(W_res) + CONSTS, writes=[ps_res])

        def transposes(src, src_res, nchunks, dst, dst_res, dst_sel=None, chunk0=0):
            def fn(e):
                last = None
                for k in range(nchunks):
                    last = e.transpose(out=tp5[:, chunk0 + k, :], in_=src[:, k * 128:(k + 1) * 128], identity=ident[:])
                return last
            P.op("tensor", fn, reads=list(src_res) + CONSTS, writes=[R_tp5])
            if dst_sel is None:
                o_ = dst[:, chunk0:chunk0 + nchunks, :]
            else:
                o_ = dst[:, chunk0:chunk0 + nchunks, dst_sel]
            P.op("scalar", lambda e: e.activation(out=o_, in_=tp5[:, chunk0:chunk0 + nchunks, :], func=AF.Copy),
                 reads=[R_tp5], writes=list(dst_res))

        def rstd_pool(ss_ap, k, inv_n, tmp_ap, out_ap, r_ss, r_tmp, r_out):
            P.op("vector", lambda e: e.tensor_scalar(out=tmp_ap, in0=ss_ap, scalar1=inv_n, scalar2=EPS, op0=ALU.mult, op1=ALU.add),
                 reads=[r_ss], writes=[r_tmp])
            P.op("gpsimd", lambda e: e.tensor_tensor(out=out_ap, in0=tmp_ap, in1=neghalf[:, 0:k], op=ALU.pow),
                 reads=[r_tmp] + CONSTS, writes=[r_out])

        def dump_and_finish():
            allres = list(R_hT) + list(R_b) + list(R_ga) + [R_decay] + ROPE
            P.op("gpsimd", lambda e: e.dma_start(out=dbg_out["d_hT"], in_=hT[:]), reads=allres, chan="dbg")
            P.op("gpsimd", lambda e: e.dma_start(out=dbg_out["d_b"], in_=b_all[:]), reads=allres, chan="dbg")
            P.op("gpsimd", lambda e: e.dma_start(out=dbg_out["d_ga"], in_=ga[:]), reads=allres, chan="dbg")
            P.op("gpsimd", lambda e: e.dma_start(out=dbg_out["d_dec"], in_=decay[:]), reads=allres, chan="dbg")
            P.op("gpsimd", lambda e: e.dma_start(out=dbg_out["d_cos"], in_=cos_t[:]), reads=allres, chan="dbg")
            P.op("gpsimd", lambda e: e.dma_start(out=dbg_out["d_sin"], in_=sin_t[:]), reads=allres, chan="dbg")
            P.final_wait("gpsimd", ["dbg"])
            P.final_wait("sync", ["fsx0", "fsx1", "fsx2", "fsx3"])
            P.emit()

        R_p1 = [Res(f"p1s{k}") for k in range(9)]

        p1_loaded = {(0, 0), (0, 1)}

        def p1_load(H, i, slots=(0, 1, 2), q="sync"):
            n = H * BPH + i
            xs = slots[i % len(slots)]
            p1_loaded.add((H, i))
            P.op(q, lambda e: e.dma_start(out=FS[xs][:], in_=x[n * 128:(n + 1) * 128, :]),
                 writes=[R_FS[xs]], chan=(f"fsx{xs}" if q == "sync" else f"fsg{xs}"))

        def p1_head(H, i, slots=(0, 1, 2), q="sync"):
            xs = slots[i % len(slots)]
            s = i % 3
            b2 = i % 2
            if (H, i) not in p1_loaded:
                p1_load(H, i, slots, q)
            P.op("scalar", lambda e: e.activation(out=HS[2 + b2][:], in_=FS[xs][:], func=AF.Square, accum_out=small[:, 40 + s:41 + s]),
                 reads=[R_FS[xs]], writes=[R_HS[2 + b2], R_p1[s]])
            P.op("scalar", lambda e: e.activation(out=small[:, 43 + s:44 + s], in_=small[:, 40 + s:41 + s], func=AF.Ln, scale=1.0 / D, bias=epsb[:, 0:1]),
                 reads=[R_p1[s], res("epsb")], writes=[R_p1[3 + s]])
            P.op("scalar", lambda e: e.activation(out=small[:, 46 + s:47 + s], in_=small[:, 43 + s:44 + s], func=AF.Exp, scale=-0.5),
                 reads=[R_p1[3 + s]], writes=[R_p1[6 + s]])

        def p1_tail(H, i, slots=(0, 1, 2)):
            xs = slots[i % len(slots)]
            s = i % 3
            b2 = i % 2
            P.op("vector", lambda e: e.scalar_tensor_tensor(out=HS[b2][:], in0=FS[xs][:], scalar=small[:, 46 + s:47 + s],
                                                            in1=normw_bc[:, 0, :], op0=ALU.mult, op1=ALU.mult),
                 reads=[R_FS[xs], R_p1[6 + s], res("normw")], writes=[R_HS[b2]])
            transposes(HS[b2], [R_HS[b2]], 8, hT, [R_hT[i]], dst_sel=slice(i * 128, (i + 1) * 128))

        def a_qproj(i):
            n_, sl = i, i % 2
            tsel = slice(i * 128, (i + 1) * 128)
            bg, Rbg = nextbig()
            Wt, Wr = wt("q")
            tm_proj(bg, Rbg, hT, tsel, [R_hT[i]], Wt, Wr, 0, 1024)
            return bg, Rbg

        def a_rope_q(n, bg, Rbg):
            q4 = bg[:].rearrange("p (h t j) -> p h t j", t=2, j=32)
            tA4 = FS[0][:].rearrange("p (h t j) -> p h t j", t=2, j=32)
            tB4 = FS[1][:].rearrange("p (h t j) -> p h t j", t=2, j=32)
            cosb = cos_t[:, n, :]
            sinb = sin_t[:, n, :]
            P.op("vector", lambda e: e.tensor_tensor(out=tA4, in0=q4, in1=cosb.unsqueeze(1).unsqueeze(1).to_broadcast([128, 16, 2, 32]), op=ALU.mult),
                 reads=[Rbg] + ROPE, writes=[R_FS[0]])

            def fn(e):
                e.scalar_tensor_tensor(out=tB4[:, :, 0, :], in0=q4[:, :, 1, :], scalar=-1.0, in1=sinb.unsqueeze(1).to_broadcast([128, 16, 32]),
                                       op0=ALU.mult, op1=ALU.mult)
                return e.tensor_tensor(out=tB4[:, :, 1, :], in0=q4[:, :, 0, :], in1=sinb.unsqueeze(1).to_broadcast([128, 16, 32]), op=ALU.mult)
            P.op("vector", fn, reads=[Rbg] + ROPE, writes=[R_FS[1]])
            P.op("gpsimd", lambda e: e.tensor_tensor(out=HS[0][:], in0=FS[0][:], in1=FS[1][:], op=ALU.add),
                 reads=[R_FS[0], R_FS[1]], writes=[R_HS[0]])

        def a_kv(i, n):
            sl = n % 3
            tsel = slice(i * 128, (i + 1) * 128)
            tm_proj(tp5f, R_tp5, hT, tsel, [R_hT[i]], WKV, R_WKV, 0, 256)
            k4 = tp5f[:, 0:128].rearrange("p (g t j) -> p g t j", t=2, j=32)
            tAk = OGT[:, 0, 0:128].rearrange("p (g t j) -> p g t j", t=2, j=32)
            tBk = OGT[:, 0, 128:256].rearrange("p (g t j) -> p g t j", t=2, j=32)
            cosb = cos_t[:, n, :]
            sinb = sin_t[:, n, :]

            def fn(e):
                e.tensor_tensor(out=tAk, in0=k4, in1=cosb.unsqueeze(1).unsqueeze(1).to_broadcast([128, 2, 2, 32]), op=ALU.mult)
                e.scalar_tensor_tensor(out=tBk[:, :, 0, :], in0=k4[:, :, 1, :], scalar=-1.0, in1=sinb.unsqueeze(1).to_broadcast([128, 2, 32]),
                                       op0=ALU.mult, op1=ALU.mult)
                e.tensor_tensor(out=tBk[:, :, 1, :], in0=k4[:, :, 0, :], in1=sinb.unsqueeze(1).to_broadcast([128, 2, 32]), op=ALU.mult)
                return e.tensor_copy(out=vaug[sl][:, :, 0:64], in_=tp5f[:, 128:256].rearrange("p (g d) -> p g d", d=64))
            P.op("vector", fn, reads=[R_tp5] + ROPE, writes=[R_OGT[0], R_vaug[sl]])

            def fn(e):
                a_ = OGT[:, 0, 0:128].rearrange("p (g d) -> p g d", d=64)
                b_ = OGT[:, 0, 128:256].rearrange("p (g d) -> p g d", d=64)
                e.tensor_tensor(out=kdup[:, :, 0, :], in0=a_, in1=b_, op=ALU.add)
                return e.tensor_tensor(out=kdup[:, :, 1, :], in0=a_, in1=b_, op=ALU.add)
            P.op("vector", fn, reads=[R_OGT[0]], writes=[R_kdup])

        def a_gate(i):
            tsel = slice(i * 128, (i + 1) * 128)
            bg, Rbg = nextbig()
            Wt, Wr = wt("ga")
            tm_proj(bg, Rbg, hT, tsel, [R_hT[i]], Wt, Wr, 0, 1024)
            P.op("scalar", lambda e: e.activation(out=FS[2][:], in_=bg[:], func=AF.Tanh, scale=0.5),
                 reads=[Rbg], writes=[R_FS[2]])
            P.op("vector", lambda e: e.scalar_tensor_tensor(out=SG[i % 2][:], in0=FS[2][:], scalar=1.0, in1=bg[:], op0=ALU.add, op1=ALU.mult),
                 reads=[R_FS[2], Rbg], writes=[R_SG[i % 2]])

        def a_mproj(i):
            tsel = slice(i * 128, (i + 1) * 128)
            bg, Rbg = nextbig()
            Wt, Wr = wt("ma")
            tm_proj(bg, Rbg, hT, tsel, [R_hT[i]], Wt, Wr, 0, 1024)
            P.op("scalar", lambda e: e.activation(out=TM[i % 2][:], in_=bg[:], func=AF.Tanh, scale=0.5),
                 reads=[Rbg], writes=[R_TM[i % 2]])

        def a_qT(i, n):
            sl = n % 2
            transposes(HS[0], [R_HS[0]], 8, fTq[sl], [R_fTq[sl]])

        def a_kT(i, n):
            sl = n % 3

            def fn(e):
                e.transpose(out=tp5[:, 0, :], in_=kdup[:, 0, :, :].rearrange("p r d -> p (r d)"), identity=ident[:])
                return e.transpose(out=tp5[:, 1, :], in_=kdup[:, 1, :, :].rearrange("p r d -> p (r d)"), identity=ident[:])
            P.op("tensor", fn, reads=[R_kdup] + CONSTS, writes=[R_tp5])
            P.op("scalar", lambda e: e.activation(out=kT2[sl][:], in_=tp5[:, 0:2, :], func=AF.Copy),
                 reads=[R_tp5], writes=[R_kT2[sl]])

        def a_qkT(i, n):
            a_qT(i, n)
            a_kT(i, n)

        def a_scores(n, qd):
            sl = n % 2
            g = qd // 2
            u = qd % 2
            h0 = qd * 4
            kbs = [1] if n == 0 else [0, 1]
            S5 = ps67[:].rearrange("p (par kb j q) -> p par kb j q", par=2, kb=2, j=2)

            def fn(e):
                last = None
                for kb in kbs:
                    ksl = n % 3 if kb == 1 else (n - 1) % 3
                    for j in range(2):
                        for par in range(2):
                            h = h0 + 2 * j + par
                            c = h // 2
                            lo = par * 64
                            last = e.matmul(S5[:, par, kb, j, :], lhsT=kT2[ksl][lo:lo + 64, g, :], rhs=fTq[sl][lo:lo + 64, c, :],
                                            start=True, stop=True)
                return last
            P.op("tensor", fn, reads=[R_kT2[n % 3], R_kT2[(n - 1) % 3], R_fTq[sl]], writes=[R_ps6, R_ps7])
            if n == 0:
                P.op("scalar", lambda e: e.activation(out=p_sb[u][:, :, 1, :, :], in_=S5[:, :, 1, :, :], func=AF.Exp, scale=0.125),
                     reads=[R_ps6, R_ps7], writes=[R_psb[u]])

                def fn(e):
                    last = None
                    for par in range(2):
                        last = e.tensor_tensor(out=p_sb[u][:, par, 1, :, :], in0=p_sb[u][:, par, 1, :, :],
                                               in1=mask2[:, 1, :].unsqueeze(1).to_broadcast([128, 2, 128]), op=ALU.mult)
                    return last
                P.op("vector", fn, reads=[R_psb[u]] + CONSTS, writes=[R_psb[u]])
            else:
                P.op("scalar", lambda e: e.activation(out=p_sb[u][:], in_=S5, func=AF.Exp, scale=0.125),
                     reads=[R_ps6, R_ps7], writes=[R_psb[u]])

                def fn(e):
                    last = None
                    for par in range(2):
                        last = e.tensor_tensor(out=p_sb[u][:, par, :, :, :], in0=p_sb[u][:, par, :, :, :],
                                               in1=mask2[:].unsqueeze(2).to_broadcast([128, 2, 2, 128]), op=ALU.mult)
                    return last
                P.op("vector", fn, reads=[R_psb[u]] + CONSTS, writes=[R_psb[u]])

        def a_pv(n, qd):
            sl = n % 2
            g = qd // 2
            u = qd % 2
            kbs = [1] if n == 0 else [0, 1]
            O4 = pk4[:, 0:260].rearrange("p (h d) -> p h d", d=65)

            def fn(e):
                last = None
                for j in range(2):
                    for par in range(2):
                        for jj, kb in enumerate(kbs):
                            ksl = n % 3 if kb == 1 else (n - 1) % 3
                            last = e.matmul(O4[:, 2 * j + par, :], lhsT=p_sb[u][:, par, kb, j, :], rhs=vaug[ksl][:, g, 0:65],
                                            start=(jj == 0), stop=(jj == len(kbs) - 1))
                return last
            P.op("tensor", fn, reads=[R_psb[u], R_vaug[n % 3], R_vaug[(n - 1) % 3]], writes=[R_pk4])
            P.op("vector", lambda e: e.tensor_tensor(out=small[:, 8:12], in0=O4[:, :, 64], in1=sinkexp[:, qd * 4:qd * 4 + 4], op=ALU.add),
                 reads=[R_pk4, res("sinkexp")], writes=[R_small[6]])
            P.op("vector", lambda e: e.reciprocal(out=small[:, 12:16], in_=small[:, 8:12]),
                 reads=[R_small[6]], writes=[R_small[7]])
            ogt = OGT[:, u, :].rearrange("p (h d) -> p h d", d=64)
            P.op("vector", lambda e: e.tensor_tensor(out=ogt, in0=O4[:, :, 0:64],
                                                     in1=small[:, 12:16].unsqueeze(2).to_broadcast([128, 4, 64]), op=ALU.mult),
                 reads=[R_pk4, R_small[7]], writes=[R_OGT[u]])
            P.op("gpsimd", lambda e: e.tensor_tensor(out=OA[sl][:, qd * 256:(qd + 1) * 256], in0=OGT[:, u, :],
                                                     in1=SG[sl][:, qd * 256:(qd + 1) * 256], op=ALU.mult),
                 reads=[R_OGT[u], R_SG[sl]], writes=[R_OA[sl]])

        def a_oT(i, n):
            transposes(OA[n % 2], [R_OA[n % 2]], 8, fTo, [R_fTo])

        def a_y(i):
            bg, Rbg = nextbig()
            Wt, Wr = wt("wa")
            tm_proj(bg, Rbg, fTo, slice(0, 128), [R_fTo], Wt, Wr, 0, 1024)
            P.op("vector", lambda e: e.scalar_tensor_tensor(out=ga[:, i, :], in0=TM[i % 2][:], scalar=1.0, in1=bg[:], op0=ALU.add, op1=ALU.mult),
                 reads=[R_TM[i % 2], Rbg], writes=[R_ga[i]])

        def a_stage1_all(i, n):
            bg, Rbg = a_qproj(i)
            a_rope_q(n, bg, Rbg)
            a_kv(i, n)
            a_gate(i)
            a_mproj(i)
            a_qkT(i, n)

        def b_ebenb(i):
            P.op("scalar", lambda e: e.activation(out=FS[0][:, 0:512], in_=b_all[:, i, :], func=AF.Exp),
                 reads=[R_b[i]], writes=[R_FS[0]])
            P.op("scalar", lambda e: e.activation(out=FS[0][:, 512:1024], in_=b_all[:, i, :], func=AF.Exp, scale=-1.0),
                 reads=[R_b[i]], writes=[R_FS[0]])

        def b_qk(i):
            tsel = slice(i * 128, (i + 1) * 128)
            bg, Rbg = nextbig()
            Wt, Wr = wt("qkb")
            tm_proj(bg, Rbg, hT, tsel, [R_hT[i]], Wt, Wr, 0, 1024)
            q_ = HS[i % 2]

            def fn(e):
                e.scalar_tensor_tensor(out=q_[:, 0:512], in0=bg[:, 0:512], scalar=128.0 ** -0.5, in1=FS[0][:, 0:512],
                                       op0=ALU.mult, op1=ALU.mult)
                return e.tensor_tensor(out=q_[:, 512:1024], in0=bg[:, 512:1024], in1=FS[0][:, 512:1024], op=ALU.mult)
            P.op("vector", fn, reads=[Rbg, R_FS[0]], writes=[R_HS[i % 2]])

        def b_v(i):
            tsel = slice(i * 128, (i + 1) * 128)
            bg, Rbg = nextbig()
            Wt, Wr = wt("vb")
            tm_proj(bg, Rbg, hT, tsel, [R_hT[i]], Wt, Wr, 0, 1024)
            P.op("scalar", lambda e: e.activation(out=HS[2 + i % 2][:], in_=bg[:], func=AF.Copy), reads=[Rbg], writes=[R_HS[2 + i % 2]])

        def b_qkT(i):
            transposes(HS[i % 2], [R_HS[i % 2]], 8, fTq[i % 2], [R_fTq[i % 2]])

        def b_gate(i):
            tsel = slice(i * 128, (i + 1) * 128)
            bg, Rbg = nextbig()
            Wt, Wr = wt("gb")
            tm_proj(bg, Rbg, hT, tsel, [R_hT[i]], Wt, Wr, 0, 1024)
            P.op("scalar", lambda e: e.activation(out=FS[0][:], in_=bg[:], func=AF.Tanh, scale=0.5),
                 reads=[Rbg], writes=[R_FS[0]])
            P.op("vector", lambda e: e.scalar_tensor_tensor(out=FS[0][:], in0=FS[0][:], scalar=1.0, in1=bg[:], op0=ALU.add, op1=ALU.mult),
                 reads=[R_FS[0], Rbg], writes=[R_FS[0]])
            P.op("gpsimd", lambda e: e.tensor_tensor(out=SG[i % 2][:].rearrange("p (h v) -> p h v", h=4), in0=FS[0][:].rearrange("p (h v) -> p h v", h=4),
                                                     in1=bnw_bc[:, 0, :].unsqueeze(1).to_broadcast([128, 4, 256]), op=ALU.mult),
                 reads=[R_FS[0], res("bnw")], writes=[R_SG[i % 2]])

        def b_att(i):
            att4 = pk4[:].rearrange("p (h i) -> p h i", h=4)
            fq = fTq[i % 2]

            def fn(e):
                last = None
                for h in range(4):
                    last = e.matmul(att4[:, h, :], lhsT=fq[:, 4 + h, :], rhs=fq[:, h, :], start=True, stop=True)
                return last
            P.op("tensor", fn, reads=[R_fTq[i % 2]], writes=[R_pk4])
            attm = kdup_att[:]
            P.op("vector", lambda e: e.tensor_tensor(out=attm, in0=att4, in1=mask2[:, 1, :].unsqueeze(1).to_broadcast([128, 4, 128]), op=ALU.mult),
                 reads=[R_pk4] + CONSTS, writes=[R_attm])

        def b_o(i, n):
            o4 = ps67[:].rearrange("p (h v) -> p h v", h=4)
            fq = fTq[i % 2]
            vs = HS[2 + i % 2]

            def fn(e):
                last = None
                for h in range(4):
                    last = e.matmul(o4[:, h, :], lhsT=kdup_att[:, h, :], rhs=vs[:, h * 256:(h + 1) * 256], start=True, stop=(n == 0))
                    if n > 0:
                        last = e.matmul(o4[:, h, :], lhsT=fq[:, h, :], rhs=S_b[:, h, :], start=False, stop=True)
                return last
            P.op("tensor", fn, reads=[R_attm, R_HS[2 + i % 2], R_fTq[i % 2], R_Sb], writes=[R_ps6, R_ps7])
            for h in range(4):
                P.op("scalar", (lambda h: lambda e: e.activation(out=OA[1][:, 0:256], in_=o4[:, h, :], func=AF.Square,
                                                                 accum_out=small[:, 16 + h:17 + h]))(h),
                     reads=[R_ps6, R_ps7], writes=[R_OA[1], R_small[0]])
            rstd_pool(small[:, 16:20], 4, 1.0 / 256.0, small[:, 20:24], small[:, 24:28], R_small[0], R_small[1], R_small[2])

        def b_inc(i, pair):
            qk_ = HS[i % 2]
            vs = HS[2 + i % 2]

            def fn(e):
                last = None
                for hh in range(2):
                    h = pair * 2 + hh
                    last = e.matmul(pk4[:, hh * 256:(hh + 1) * 256], lhsT=qk_[:, 512 + h * 128:512 + (h + 1) * 128],
                                    rhs=vs[:, h * 256:(h + 1) * 256], start=True, stop=True)
                return last
            P.op("tensor", fn, reads=[R_HS[i % 2], R_HS[2 + i % 2]], writes=[R_pk4])
            P.op("vector", lambda e: e.tensor_tensor(out=FS[2][:, pair * 512:(pair + 1) * 512],
                                                     in0=S_f[:, pair * 2:pair * 2 + 2, :].rearrange("p h v -> p (h v)"), in1=pk4[:], op=ALU.add),
                 reads=[R_Sf, R_pk4], writes=[R_FS[2]])

        def b_supd(i):
            def fn(e):
                dcb = decay[:, i * 4:(i + 1) * 4].unsqueeze(2).to_broadcast([128, 4, 256])
                e.tensor_tensor(out=S_f[:], in0=FS[2][:].rearrange("p (h v) -> p h v", h=4), in1=dcb, op=ALU.mult)
                return e.tensor_tensor(out=S_b[:], in0=FS[2][:].rearrange("p (h v) -> p h v", h=4), in1=dcb, op=ALU.mult)
            P.op("gpsimd", fn, reads=[R_FS[2], R_decay], writes=[R_Sf, R_Sb])

        def b_onorm(i):
            o4 = ps67[:].rearrange("p (h v) -> p h v", h=4)
            P.op("vector", lambda e: e.tensor_tensor(out=FS[1][:].rearrange("p (h v) -> p h v", h=4), in0=o4,
                                                     in1=small[:, 24:28].unsqueeze(2).to_broadcast([128, 4, 256]), op=ALU.mult),
                 reads=[R_ps6, R_ps7, R_small[2]], writes=[R_FS[1]])
            P.op("vector", lambda e: e.tensor_tensor(out=OA[0][:], in0=FS[1][:], in1=SG[i % 2][:], op=ALU.mult),
                 reads=[R_FS[1], R_SG[i % 2]], writes=[R_OA[0]])

        bm_state = {}

        def b_m_proj(i):
            tsel = slice(i * 128, (i + 1) * 128)
            bg, Rbg = nextbig()
            Wt, Wr = wt("mb")
            tm_proj(bg, Rbg, hT, tsel, [R_hT[i]], Wt, Wr, 0, 1024)
            bm_state["bg"] = (bg, Rbg)

        def b_m_tanh(i):
            bg, Rbg = bm_state["bg"]
            P.op("scalar", lambda e: e.activation(out=TM[0][:], in_=bg[:], func=AF.Tanh, scale=0.5),
                 reads=[Rbg], writes=[R_TM[0]])

        def b_yT(i):
            transposes(OA[0], [R_OA[0]], 8, fTo, [R_fTo])

        def b_y(i):
            bg, Rbg = nextbig()
            Wt, Wr = wt("wb")
            tm_proj(bg, Rbg, fTo, slice(0, 128), [R_fTo], Wt, Wr, 0, 1024)
            P.op("vector", lambda e: e.scalar_tensor_tensor(out=FS[1][:], in0=TM[0][:], scalar=1.0, in1=bg[:], op0=ALU.add, op1=ALU.mult),
                 reads=[R_TM[0], Rbg], writes=[R_FS[1]])
            P.op("gpsimd", lambda e: e.tensor_tensor(out=ga[:, i, :], in0=FS[1][:], in1=ga[:, i, :], op=ALU.add),
                 reads=[R_FS[1], R_ga[i]], writes=[R_ga[i]])

        R_p1b = [Res("p1b0"), Res("p1b1"), Res("p1b2")]

        def p1b_load(H1, i):
            n = H1 * BPH + i
            P.op("sync", lambda e: e.dma_start(out=FS[3][:], in_=x[n * 128:(n + 1) * 128, :]),
                 writes=[R_FS[3]], chan="fsx3")

        def p1b_front(H1, i):
            P.op("scalar", lambda e: e.activation(out=OA[1][:], in_=FS[3][:], func=AF.Square, accum_out=small[:, 50:51]),
                 reads=[R_FS[3]], writes=[R_OA[1], R_p1b[0]])
            rstd_pool(small[:, 50:51], 1, 1.0 / D, small[:, 51:52], small[:, 52:53], R_p1b[0], R_p1b[1], R_p1b[2])
            P.op("vector", lambda e: e.scalar_tensor_tensor(out=TM[1][:], in0=FS[3][:], scalar=small[:, 52:53],
                                                            in1=normw_bc[:, 0, :], op0=ALU.mult, op1=ALU.mult),
                 reads=[R_FS[3], R_p1b[2], res("normw")], writes=[R_TM[1]])
            if i + 1 < BPH:
                p1b_load(H1, i + 1)

        def p1b_back(H1, i):
            transposes(TM[1], [R_TM[1]], 8, hT, [R_hT[i]], dst_sel=slice(i * 128, (i + 1) * 128))

        cT_done = set()

        def c_T(H, i):
            if (H, i) in cT_done:
                return
            cT_done.add((H, i))
            n = H * BPH + i
            b2 = i % 2
            xs = i % 3
            P.op("sync", lambda e: e.dma_start(out=FS[xs][:], in_=x[n * 128:(n + 1) * 128, :]),
                 writes=[R_FS[xs]], chan=f"fsx{xs}")
            transposes(ga[:, i, :], [R_ga[i]], 8, fTq[b2], [R_fTq[b2]])

        def c_P(H, i):
            b2 = i % 2
            Wt, Wr = wt("wo")
            tm_proj(bigs[b2], R_bigs[b2], fTq[b2], slice(0, 128), [R_fTq[b2]], Wt, Wr, 0, 1024)

        def c_tail(H, i):
            n = H * BPH + i
            b2 = i % 2
            s = i % 3
            P.op("vector", lambda e: e.scalar_tensor_tensor(out=FS[s][:], in0=bigs[b2][:], scalar=0.25, in1=FS[s][:], op0=ALU.mult, op1=ALU.add),
                 reads=[R_bigs[b2], R_FS[s]], writes=[R_FS[s]])
            P.op("scalar", lambda e: e.activation(out=OA[1][:], in_=FS[s][:], func=AF.Square, accum_out=small[:, 32 + b2:33 + b2]),
                 reads=[R_FS[s]], writes=[R_OA[1], R_small[b2]])
            P.op("scalar", lambda e: e.activation(out=small[:, 34 + b2:35 + b2], in_=small[:, 32 + b2:33 + b2], func=AF.Ln, scale=1.0 / D, bias=epsb[:, 0:1]),
                 reads=[R_small[b2], res("epsb")], writes=[R_small[2 + b2]])
            P.op("scalar", lambda e: e.activation(out=small[:, 36 + b2:37 + b2], in_=small[:, 34 + b2:35 + b2], func=AF.Exp, scale=-0.5),
                 reads=[R_small[2 + b2]], writes=[R_small[4 + b2]])
            P.op("vector", lambda e: e.scalar_tensor_tensor(out=FS[s][:], in0=FS[s][:], scalar=small[:, 36 + b2:37 + b2], in1=fnw_bc[:, 0, :],
                                                            op0=ALU.mult, op1=ALU.mult),
                 reads=[R_FS[s], R_small[4 + b2], res("fnw")], writes=[R_FS[s]])
            P.op("sync", lambda e: e.dma_start(out=out[n * 128:(n + 1) * 128, :], in_=FS[s][:]),
                 reads=[R_FS[s]], writes=[R_FS[s]], chan=f"fsx{s}")

        epsb = sb("epsb", [128, 1], F32)
        P.op("gpsimd", lambda e: e.memset(epsb[:], EPS), writes=[res("epsb")])
        kdup_att = sb("attm", [128, 4, 128], BF16)
        R_attm = Res("attm")

        p2_early = set()

        def p2_blT(t, pst, Rpst):
            if t == 0:
                P.op("gpsimd", lambda e: e.memset(FS[3][0:32, :], 1.0), writes=[R_FS[3]])

            def fn(e):
                last = None
                for c in range(8):
                    last = e.matmul(pst[0:16, 0:512], lhsT=wlow[:, c, :], rhs=hT[:, c, t * 512:(t + 1) * 512],
                                    start=(c == 0), stop=(c == 7))
                return last
            P.op("tensor", fn, reads=R_hT[t * 4:(t + 1) * 4] + [res("wlow")], writes=[Rpst])
            P.op("vector", lambda e: e.tensor_copy(out=blT[0:16, t * 512:(t + 1) * 512], in_=pst[0:16, 0:512]),
                 reads=[Rpst], writes=[R_blT])

        def p2_pieces():
            flb = [OGT[:].rearrange("p u d -> p (u d)"),
                   p_sb[0][:].rearrange("p a b c d -> p (a b c d)").bitcast(F32)]
            Rflb = [[R_OGT[0], R_OGT[1]], [R_psb[0]]]

            def x_(i):
                pg = ps6 if i % 2 == 0 else ps7
                Rpg = R_ps6 if i % 2 == 0 else R_ps7
                fl, Rw = flb[i % 2], Rflb[i % 2]
                P.op("tensor", lambda e: e.matmul(pg[:, 0:512], lhsT=blT[0:17, i * 128:(i + 1) * 128], rhs=gup_sb[0:17, :], start=True, stop=True),
                     reads=[R_blT, res("gup")], writes=[Rpg])
                P.op("scalar", lambda e: e.activation(out=fl, in_=pg[:, 0:512], func=AF.Exp, scale=-1.0),
                     reads=[Rpg], writes=Rw)
                P.op("scalar", lambda e: e.activation(out=fl, in_=fl, func=AF.Ln, bias=1.0),
                     reads=Rw, writes=Rw)

            def y_(i):
                pg = ps6 if i % 2 == 0 else ps7
                Rpg = R_ps6 if i % 2 == 0 else R_ps7
                fl, Rw = flb[i % 2], Rflb[i % 2]
                P.op("tensor", lambda e: e.matmul(pg[:, 0:512], lhsT=triL[:], rhs=fl, start=True, stop=True),
                     reads=Rw + CONSTS, writes=[Rpg])
                P.op("scalar", lambda e: e.activation(out=b_all[:, i, :], in_=pg[:, 0:512], func=AF.Copy),
                     reads=[Rpg], writes=[R_b[i]])

                def fn(e):
                    last = None
                    for h in range(4):
                        cc = (i * 4 + h) * 2
                        last = e.matmul(pk4[:, cc:cc + 2], lhsT=fl[:, h * 128:(h + 1) * 128], rhs=triL[:, 126:128],
                                        start=True, stop=True)
                    return last
                P.op("tensor", fn, reads=Rw + CONSTS, writes=[R_pk4])

            def fin_():
                P.op("scalar", lambda e: e.activation(out=decay[:], in_=pk4[:, 0:BPH * 8].rearrange("p (m two) -> p m two", two=2)[:, :, 1],
                                                      func=AF.Exp),
                     reads=[R_pk4], writes=[R_decay])
            return x_, y_, fin_

        def emit_phase2(hosted=None):
            alt = hosted is not None
            if alt:
                flb = [OGT[:].rearrange("p u d -> p (u d)"),
                       p_sb[0][:].rearrange("p a b c d -> p (a b c d)").bitcast(F32)]
                Rflb = [[R_OGT[0], R_OGT[1]], [R_psb[0]]]
            else:
                flb = [FS[2][:, 0:512], FS[2][:, 512:1024]]
                Rflb = [[res("fl0"), R_FS[2]], [res("fl1"), R_FS[2]]]
            for t in range(BPH * 128 // 512):
                if t not in p2_early:
                    p2_blT(t, pk4, R_pk4)
            p2_early.clear()

            def p2_x(i):
                pg = ps6 if i % 2 == 0 else ps7
                Rpg = R_ps6 if i % 2 == 0 else R_ps7
                fl = flb[i % 2]
                Rw = Rflb[i % 2]
                Rr = Rw[:1] if not alt else Rw
                P.op("tensor", lambda e: e.matmul(pg[:, 0:512], lhsT=blT[0:17, i * 128:(i + 1) * 128], rhs=gup_sb[0:17, :], start=True, stop=True),
                     reads=[R_blT, res("gup")], writes=[Rpg])
                P.op("scalar", lambda e: e.activation(out=fl, in_=pg[:, 0:512], func=AF.Exp, scale=-1.0),
                     reads=[Rpg], writes=Rw)
                P.op("scalar", lambda e: e.activation(out=fl, in_=fl, func=AF.Ln, bias=1.0),
                     reads=Rr, writes=Rw)

            def p2_y(i):
                pg = ps6 if i % 2 == 0 else ps7
                Rpg = R_ps6 if i % 2 == 0 else R_ps7
                fl = flb[i % 2]
                Rw = Rflb[i % 2]
                Rr = Rw[:1] if not alt else Rw
                P.op("tensor", lambda e: e.matmul(pg[:, 0:512], lhsT=triL[:], rhs=fl, start=True, stop=True),
                     reads=Rr + CONSTS, writes=[Rpg])
                P.op("scalar", lambda e: e.activation(out=b_all[:, i, :], in_=pg[:, 0:512], func=AF.Copy),
                     reads=[Rpg], writes=[R_b[i]])

                def fn(e):
                    last = None
                    for h in range(4):
                        cc = (i * 4 + h) * 2
                        last = e.matmul(pk4[:, cc:cc + 2], lhsT=fl[:, h * 128:(h + 1) * 128], rhs=triL[:, 126:128],
                                        start=True, stop=True)
                    return last
                P.op("tensor", fn, reads=Rr + CONSTS, writes=[R_pk4])

            if not alt:
                for k_ in range(2):
                    r_ = res(f"fl{k_}")
                    r_.w = R_FS[2].w
                    r_.r = dict(R_FS[2].r)
            else:
                c_T(hosted, 0)
                c_T(hosted, 1)
            p2_x(0)
            for i in range(BPH):
                if alt:
                    c_P(hosted, i)
                if i + 1 < BPH:
                    p2_x(i + 1)
                if alt and i + 2 < BPH:
                    c_T(hosted, i + 2)
                p2_y(i)
                if alt:
                    c_tail(hosted, i)
            if not alt:
                for k_ in range(2):
                    for c_, v_ in res(f"fl{k_}").r.items():
                        R_FS[2].r[c_] = max(R_FS[2].r.get(c_, 0), v_)
            P.op("scalar", lambda e: e.activation(out=decay[:], in_=pk4[:, 0:BPH * 8].rearrange("p (m two) -> p m two", two=2)[:, :, 1],
                                                  func=AF.Exp),
                 reads=[R_pk4], writes=[R_decay])


        done = False
        for H in range(NHALF):
            blk0 = H * BPH
            if H == 0:
                wload("q", w_in_v, O_AQ)
                load_group(WKV, R_WKV, w_in_v, O_AK, 256, "wkv")
                wload("ga", w_in_v, O_AG)
                wload("ma", w_in_v, O_MA)
                wload("wa", w_a_v, 0)
                p2x, p2y, p2fin = p2_pieces()
                p1_head(H, 0)
                p1_head(H, 1)
                for i in range(BPH):
                    if i + 2 < BPH:
                        p1_head(H, i + 2)
                    p1_tail(H, i)
                    if i == 3:
                        p2_blT(0, pk4, R_pk4)
                        p2x(0)
                    elif 4 <= i <= 6:
                        p2x(i - 3)
                        p2y(i - 4)
                    elif i == 7:
                        p2_blT(1, tp5f, R_tp5)
                        p2y(3)
                        p2x(4)
                emit_rope_tables()
                bg0, Rbg0 = a_qproj(0)
                a_rope_q(0, bg0, Rbg0)
                p2x(5); p2y(4)
                a_kv(0, 0)
                p2x(6); p2y(5)
                a_qkT(0, 0)
                p2x(7); p2y(6)
                p2y(7)
                p2fin()
            if dbg == "p2":
                dump_and_finish(); done = True; break

            if H != 0:
                a_stage1_all(0, blk0)
            for i in range(BPH):
                n = blk0 + i
                nxt = i + 1 < BPH
                prv = i - 1 >= 0
                last = not nxt
                a_scores(n, 0)
                if H == 0 and i == 0:
                    a_gate(0)
                if nxt:
                    bg, Rbg = a_qproj(i + 1)
                    a_rope_q(n + 1, bg, Rbg)
                    if i == BPH - 2:
                        wfree("q")
                        wload("vb", w_in_v, O_BV)
                if last:
                    if H + 1 < NHALF:
                        p1b_load(H + 1, 0)
                    b_ebenb(0)
                    b_qk(0)
                if prv:
                    a_oT(i - 1, n - 1)
                a_pv(n, 0)
                a_scores(n, 1)
                if H == 0 and i == 0:
                    a_mproj(0)
                if nxt:
                    a_kv(i + 1, n + 1)
                if prv:
                    a_y(i - 1)
                a_pv(n, 1)
                a_scores(n, 2)
                if nxt:
                    a_qT(i + 1, n + 1)
                    a_gate(i + 1)
                    if i == BPH - 2:
                        wfree("ga")
                        wload("gb", w_in_v, O_BG)
                    a_kT(i + 1, n + 1)
                if last:
                    b_qkT(0)
                a_pv(n, 2)
                a_scores(n, 3)
                if nxt:
                    a_mproj(i + 1)
                    if i == BPH - 2:
                        wfree("ma")
                        wload("mb", w_in_v, O_MB)
                if last:
                    b_v(0)
                a_pv(n, 3)
                if H == 0 and i in (2, 5):
                    flush_stores()
                if i == 1:
                    wload("qkb", w_in_v, O_BQ)
            a_oT(BPH - 1, blk0 + BPH - 1)
            b_gate(0)
            a_y(BPH - 1)
            wfree("wa")
            wload("wb", w_b_v, 0)
            if dbg == "A":
                dump_and_finish(); done = True; break

            import os
            BO = os.environ.get("B_ORDER", "")
            for i in range(BPH):
                n = blk0 + i
                nxt = i + 1 < BPH
                b_att(i)
                if nxt and "1" not in BO:
                    b_ebenb(i + 1)
                    b_qk(i + 1)
                    if i == BPH - 2:
                        wfree("qkb")
                        wload("wo", w_o_v, 0)
                b_o(i, n)
                if not nxt:
                    b_m_proj(i)
                if n < NBLK - 1:
                    b_inc(i, 0)
                if not nxt:
                    c_T(H, 0)
                if nxt and "2" not in BO:
                    b_v(i + 1)
                    if i == BPH - 2:
                        wfree("vb")
                        if H + 1 < NHALF:
                            wload("q", w_in_v, O_AQ)
                if n < NBLK - 1:
                    b_inc(i, 1)
                    b_supd(i)
                b_onorm(i)
                if H + 1 < NHALF:
                    p1b_front(H + 1, i)
                    if not nxt:
                        bgf, Rbgf = nextbig()
                        p2_blT(0, bgf, Rbgf)
                        p2_early.add(0)
                if nxt and "3" not in BO:
                    b_qkT(i + 1)
                if nxt and "4" not in BO:
                    b_gate(i + 1)
                    if i == BPH - 2:
                        wfree("gb")
                        if H + 1 < NHALF:
                            wload("ga", w_in_v, O_AG)
                b_yT(i)
                if nxt:
                    b_m_proj(i)
                b_m_tanh(i)
                b_y(i)
                if H + 1 < NHALF:
                    p1b_back(H + 1, i)
                if H == 0 and i == 2:
                    flush_stores()
                if nxt and "1" in BO:
                    b_ebenb(i + 1)
                    b_qk(i + 1)
                if nxt and "2" in BO:
                    b_v(i + 1)
                if nxt and "3" in BO:
                    b_qkT(i + 1)
                if nxt and "4" in BO:
                    b_gate(i + 1)
            wfree("mb"); wfree("wb")
            if dbg == "B":
                dump_and_finish(); done = True; break

            if H + 1 < NHALF:
                wload("ma", w_in_v, O_MA)
                wload("wa", w_a_v, 0)
                emit_phase2(hosted=H)
                flush_stores()
            else:
                c_T(H, 0)
                c_T(H, 1)
                for i in range(BPH):
                    c_P(H, i)
                    if i + 2 < BPH:
                        c_T(H, i + 2)
                    c_tail(H, i)
            wfree("wo")
            if dbg == "C":
                dump_and_finish(); done = True; break

        if not done:
            P.final_wait("sync", ["fsx0", "fsx1", "fsx2", "fsx3"])
            P.emit()
    return nc


_NC_CACHE = {}


def _get_nc():
    if "nc" not in _NC_CACHE:
        _NC_CACHE["nc"] = build()
    return _NC_CACHE["nc"]


def _in_maps(x, positions, norm_w, w_in, a_sinks, b_gate_up, b_gate_bias, b_out_norm_w,
             w_a_proj, w_b_proj, w_out, final_norm_w):
    f32 = np.float32
    invf = (10000.0 ** (-np.arange(32, dtype=np.float32) / np.float32(32))).astype(f32)
    invf_b = np.ascontiguousarray(np.broadcast_to(invf[None, :], (128, 32)))
    gup = np.ascontiguousarray(np.concatenate([np.asarray(b_gate_up[0], f32), np.asarray(b_gate_bias[0], f32)[None, :]], axis=0))
    shared = {
        "invf": invf_b,
        "norm_w": np.ascontiguousarray(np.asarray(norm_w, f32).reshape(1, D)),
        "fnorm_w": np.ascontiguousarray(np.asarray(final_norm_w, f32).reshape(1, D)),
        "bnw": np.ascontiguousarray(np.asarray(b_out_norm_w, f32).reshape(1, 256)),
        "sinks": np.ascontiguousarray(np.asarray(a_sinks, f32).reshape(1, 16)),
        "gup": gup,
        "w_in": np.ascontiguousarray(np.asarray(w_in[0], f32)),
        "w_a": np.ascontiguousarray(np.asarray(w_a_proj[0], f32)),
        "w_b": np.ascontiguousarray(np.asarray(w_b_proj[0], f32)),
        "w_o": np.ascontiguousarray(np.asarray(w_out[0], f32)),
    }
    maps = []
    for b in range(8):
        m = dict(shared)
        m["x"] = np.ascontiguousarray(np.asarray(x[b], f32))
        m["pos"] = np.ascontiguousarray(np.asarray(positions[b], np.int32).reshape(NBLK, 128).T)
        maps.append(m)
    return maps


def kernel(x, positions, norm_w, w_in, a_sinks, b_gate_up, b_gate_bias, b_out_norm_w,
           w_a_proj, w_b_proj, w_out, final_norm_w):
    nc = _get_nc()
    maps = _in_maps(x, positions, norm_w, w_in, a_sinks, b_gate_up, b_gate_bias, b_out_norm_w,
                    w_a_proj, w_b_proj, w_out, final_norm_w)
    res_ = run_bass_kernel_spmd(nc, maps, core_ids=list(range(8)))
    return np.stack([np.asarray(r["out"], np.float32) for r in res_.results], axis=0)
```
